# Optimizing a Trainium2 kernel written in Bass

```python
import math
import jax, jax.numpy as jnp
from jax import lax
import numpy as np

D_MODEL = 1024
BATCH = 32
SEQ = 2048
DEPTH = 1

CHUNK = 64
LEFT_CHUNKS = 8
BAND = (LEFT_CHUNKS + 1) * CHUNK
MIX_W = D_MODEL
GROUP_W = MIX_W // 2
ATT_HEADS = 8
HEAD_DIM = GROUP_W // ATT_HEADS
CONV_GROUPS = 8
CONV_WIDTH = 3
MAX_REL = 128
IN_COLS = 8 * GROUP_W
LN_EPS = 1e-5
ALPHA = (2.0 * DEPTH) ** 0.25
BETA = (8.0 * DEPTH) ** -0.25

kernel_name = "hybrid_chunkattn_shortconv_deepnorm"


def layer_norm(x, g, b):
    xf = x.astype(jnp.float32)
    mu = jnp.mean(xf, axis=-1, keepdims=True)
    var = jnp.mean(jnp.square(xf - mu), axis=-1, keepdims=True)
    y = (xf - mu) * lax.rsqrt(var + LN_EPS)
    return (y * g.astype(jnp.float32) + b.astype(jnp.float32)).astype(x.dtype)


def chunk_attention(q, k, v, rel_bias):
    bsz, seq, nh, dh = q.shape
    n_chunks = seq // CHUNK
    scale = 1.0 / math.sqrt(dh)
    qi = np.arange(CHUNK)[:, None] + LEFT_CHUNKS * CHUNK
    kj = np.arange(BAND)[None, :]
    rel_idx = np.clip(qi - kj, -MAX_REL, MAX_REL) + MAX_REL
    bias = rel_bias.astype(jnp.float32)[:, rel_idx]
    pad = ((0, 0), (LEFT_CHUNKS * CHUNK, 0), (0, 0), (0, 0))
    kp = jnp.pad(k, pad)
    vp = jnp.pad(v, pad)
    qc = q.reshape(bsz, n_chunks, CHUNK, nh, dh).transpose(1, 0, 2, 3, 4)
    band_off = jnp.arange(BAND) - LEFT_CHUNKS * CHUNK

    def one_chunk(args):
        c, qb = args
        start = c * CHUNK
        kb = lax.dynamic_slice_in_dim(kp, start, BAND, axis=1)
        vb = lax.dynamic_slice_in_dim(vp, start, BAND, axis=1)
        s = jnp.einsum('bqhd,bkhd->bhqk', qb, kb).astype(jnp.float32) * scale + bias[None]
        valid = (start + band_off) >= 0
        s = jnp.where(valid[None, None, None, :], s, jnp.float32(-1e30))
        p = jax.nn.softmax(s, axis=-1).astype(vb.dtype)
        return jnp.einsum('bhqk,bkhd->bqhd', p, vb)

    out = lax.map(one_chunk, (jnp.arange(n_chunks), qc))
    return out.transpose(1, 0, 2, 3, 4).reshape(bsz, seq, nh * dh)


def short_gated_conv(bg, cg, h, conv_w, conv_b):
    u = cg * h
    up = jnp.pad(u, ((0, 0), (CONV_WIDTH - 1, 0), (0, 0)))
    seq = u.shape[1]
    conv = conv_b + sum(conv_w[i] * up[:, i:i + seq] for i in range(CONV_WIDTH))
    return bg * conv


def mixer_layer(x, w_in, rel_bias, conv_w, conv_b, w_out):
    bsz, seq, _ = x.shape
    proj = jnp.einsum('bsd,de->bse', x, w_in)
    q, k, v, z_a, bg, cg, h, z_c = jnp.split(proj, 8, axis=-1)
    heads = lambda t: t.reshape(bsz, seq, ATT_HEADS, HEAD_DIM)
    y_att = chunk_attention(heads(q), heads(k), heads(v), rel_bias) * jax.nn.silu(z_a)
    y_conv = short_gated_conv(bg, cg, h, conv_w, conv_b) * jax.nn.silu(z_c)
    y = jnp.concatenate([y_att, y_conv], axis=-1)
    return jnp.einsum('bse,ed->bsd', y, w_out)


def setup_inputs(seed: int = 0) -> dict:
    key = jax.random.key(seed)
    ks = jax.random.split(key, 12)
    f32 = jnp.float32
    x = jax.random.normal(ks[0], (BATCH, SEQ, D_MODEL), f32)
    ln0_g = 1.0 + 0.01 * jax.random.normal(ks[1], (D_MODEL,), f32)
    ln0_b = 0.01 * jax.random.normal(ks[2], (D_MODEL,), f32)
    w_in = jax.random.normal(ks[3], (DEPTH, D_MODEL, IN_COLS), f32) * D_MODEL ** -0.5
    rel_bias = 0.1 * jax.random.normal(ks[4], (DEPTH, ATT_HEADS, 2 * MAX_REL + 1), f32)
    conv_w = jax.random.normal(ks[5], (DEPTH, CONV_WIDTH, GROUP_W), f32) * CONV_WIDTH ** -0.5
    conv_b = 0.01 * jax.random.normal(ks[6], (DEPTH, GROUP_W), f32)
    w_out = jax.random.normal(ks[7], (DEPTH, MIX_W, D_MODEL), f32) * (MIX_W ** -0.5) * BETA
    ln_g = 1.0 + 0.01 * jax.random.normal(ks[8], (DEPTH, D_MODEL), f32)
    ln_b = 0.01 * jax.random.normal(ks[9], (DEPTH, D_MODEL), f32)
    return {"x": x, "ln0_g": ln0_g, "ln0_b": ln0_b, "w_in": w_in, "rel_bias": rel_bias,
            "conv_w": conv_w, "conv_b": conv_b, "w_out": w_out, "ln_g": ln_g, "ln_b": ln_b}


def reference(x, ln0_g, ln0_b, w_in, rel_bias, conv_w, conv_b, w_out, ln_g, ln_b):
    x = layer_norm(x, ln0_g, ln0_b)
    for l in range(DEPTH):
        h = mixer_layer(x, w_in[l], rel_bias[l], conv_w[l], conv_b[l], w_out[l])
        x = layer_norm(ALPHA * x + h, ln_g[l], ln_b[l])
    return x
```

```python
import numpy as np
from contextlib import ExitStack

import concourse.bass as bass
import concourse.mybir as mybir
from concourse.bass_utils import run_bass_kernel_spmd

F32 = mybir.dt.float32
BF16 = mybir.dt.bfloat16
ALU = mybir.AluOpType
AF = mybir.ActivationFunctionType

N_CORES = 8
D = 1024
NCOLS = 4096
H = 8
DH = 64
NTB = 512
TPB = 4
LN_EPS = 1e-5
ALPHA = 2.0 ** 0.25
NSLOT = 8
ENGS = ("pe", "act", "dve", "pool", "sp")

LO = [max(0, 2 * j - 8) for j in range(8)]
HI = [min(7, 2 * j + 1) for j in range(8)]
NQ = [HI[j] - LO[j] + 1 for j in range(8)]
OFF = [64 * sum(NQ[:j]) for j in range(8)]
PTW = 64 * sum(NQ)


class Prog:
    def __init__(self):
        self.q = {e: [] for e in ENGS}
        self.res = {}
        self.dma_cnt = {}

    def _deps(self, reads, writes):
        deps = []
        for r in reads:
            st = self.res.get(r)
            if st and st[0] is not None:
                deps.append(("raw", st[0]))
        for w in writes:
            st = self.res.get(w)
            if st:
                if st[0] is not None:
                    deps.append(("waw", st[0]))
                for ev in st[1].values():
                    deps.append(("war", ev))
        return deps

    def _commit(self, ev, reads, writes):
        for r in reads:
            st = self.res.setdefault(r, [None, {}])
            st[1][ev[1]] = ev
        for w in writes:
            self.res[w] = [ev, {}]

    def op(self, eng, fn, reads=(), writes=()):
        idx = len(self.q[eng])
        self.q[eng].append(dict(fn=fn, deps=self._deps(reads, writes), mark=False, dma=None,
                                tag="%s<-%s" % (",".join(writes), ",".join(reads))))
        self._commit(("e", eng, idx), reads, writes)

    def dma(self, eng, fn, key, reads=(), writes=()):
        val = self.dma_cnt.get(key, 0) + 16
        self.dma_cnt[key] = val
        self.q[eng].append(dict(fn=fn, deps=self._deps(reads, writes), mark=False, dma=(key, val)))
        self._commit(("d", key, val), reads, writes)

    def fix_dma_key(self, key):
        tot = self.dma_cnt[key]
        for st in self.res.values():
            if st[0] is not None and st[0][0] == "d" and st[0][1] == key:
                st[0] = ("d", key, tot)

    def wait_all(self, eng, names):
        deps = []
        for n in names:
            st = self.res.get(n)
            if st:
                if st[0] is not None:
                    deps.append(("raw", st[0]))
                for ev in st[1].values():
                    deps.append(("war", ev))
        self.q[eng].append(dict(fn=None, deps=deps, mark=False, dma=None))

    @staticmethod
    def _skip(kind, ev, e):
        return ev[0] == "e" and ev[1] == e and e == "pe"

    def emit(self, nc, es):
        for e in ENGS:
            for ins in self.q[e]:
                for kind, ev in ins["deps"]:
                    if ev[0] == "e" and not self._skip(kind, ev, e):
                        self.q[ev[1]][ev[2]]["mark"] = True
        for e in ENGS:
            c = 0
            for ins in self.q[e]:
                if ins["mark"]:
                    c += 1
                    ins["cnt"] = c
        esem = {e: es.enter_context(nc.semaphore("s_" + e)) for e in ENGS}
        dsem = {k: es.enter_context(nc.semaphore("d_" + k)) for k in self.dma_cnt}
        stats = {e: [len(self.q[e]), 0] for e in ENGS}

        def run(e, eng):
            seen = {}
            for ins in self.q[e]:
                need = {}
                for kind, ev in ins["deps"]:
                    if self._skip(kind, ev, e):
                        continue
                    if ev[0] == "e":
                        k = ("e", ev[1])
                        v = self.q[ev[1]][ev[2]]["cnt"]
                    else:
                        k = ("d", ev[1])
                        v = ev[2]
                    if v > need.get(k, 0):
                        need[k] = v
                for k, v in need.items():
                    if seen.get(k, 0) >= v:
                        continue
                    seen[k] = v
                    eng.wait_ge(esem[k[1]] if k[0] == "e" else dsem[k[1]], v)
                    stats[e][1] += 1
                if ins["fn"] is None:
                    continue
                r = ins["fn"](eng)
                if ins["dma"] is not None:
                    r.then_inc(dsem[ins["dma"][0]], 16)
                elif ins["mark"]:
                    r.then_inc(esem[e], 1)

        block = es.enter_context(nc.Block())

        @block.tensor
        def _(eng):
            run("pe", eng)

        @block.scalar
        def _(eng):
            run("act", eng)

        @block.vector
        def _(eng):
            run("dve", eng)

        @block.gpsimd
        def _(eng):
            run("pool", eng)

        @block.sync
        def _(eng):
            run("sp", eng)

        return stats


def build(nseq, seqlen):
    ntok = nseq * seqlen
    nbs = seqlen // NTB
    nblk = nseq * nbs
    ntile = ntok // 128

    nc = bass.Bass("TRN2", target_bir_lowering=False)
    es = ExitStack()

    def din(name, shape):
        return nc.dram_tensor(name, shape, F32, kind="ExternalInput").ap()

    x_d = din("x", [ntok, D]).rearrange("(n p) d -> n p d", p=128)
    w_in_d = din("w_in", [D, NCOLS])
    w_out_d = din("w_out", [D, D])
    g0T_d = din("g0T", [128, 8])
    b0T_d = din("b0T", [128, 8])
    g0bc_d = din("g0bc", [128, D])
    b0bc_d = din("b0bc", [128, D])
    g1bc_d = din("g1bc", [128, D])
    b1bc_d = din("b1bc", [128, D])
    cw_d = din("cw", [128, 12])
    cb_d = din("cb", [128, 4])
    rtab_d = din("rtab", [128, 2048])
    rc_d = din("rc", [128, 8])
    out_d = nc.dram_tensor("out", [ntok, D], F32, kind="ExternalOutput").ap().rearrange(
        "(n p) d -> n p d", p=128)

    def sb(name, shape, dt):
        return es.enter_context(nc.sbuf_tensor(name, shape, dt))

    def ps(name, shape, dt):
        return es.enter_context(nc.psum_tensor(name, shape, dt))

    w_in_sb = sb("w_in_sb", [128, 8, NCOLS], BF16)
    w_out_sb = sb("w_out_sb", [128, 8, D], BF16)
    ag0 = sb("ag0", [128, D], F32)
    ab0 = sb("ab0", [128, D], F32)
    g1bc = sb("g1bc_sb", [128, D], F32)
    b1bc = sb("b1bc_sb", [128, D], F32)
    xs = sb("xs", [128, NSLOT, D], F32)
    xhb = sb("xhb", [128, TPB, D], BF16)
    xnT = sb("xnT", [128, 8, NTB], BF16)
    qz = sb("qz", [128, 4, 2, NTB], BF16)
    kT = sb("kT", [128, 2, 4, NTB], BF16)
    vA = sb("vA", [128, 2, TPB, H, 65], BF16)
    PT = sb("PT", [128, 2, PTW], BF16)
    etab = sb("etab", [128, H, 256], BF16)
    gate = sb("gate", [128, TPB, 512], BF16)
    ytok = sb("ytok", [128, TPB, 512], BF16)
    ynorm = sb("ynorm", [128, TPB, 64], F32)
    yT = sb("yT", [128, 8, NTB], BF16)
    ubuf = sb("ubuf", [128, NTB + 2], F32)
    acc = sb("acc", [128, NTB], F32)
    tz = sb("tz", [128, NTB], F32)
    halo = sb("halo", [128, 4, 2], F32)
    ident = sb("ident", [128, 128], BF16)
    mk = sb("mk", [128, 64], BF16)
    g0T = sb("g0T_sb", [128, 8], F32)
    b0T = sb("b0T_sb", [128, 8], F32)
    cwh = sb("cwh", [128, 12], F32)
    cbh = sb("cbh", [128, 4], F32)
    nrc = sb("nrc", [128, 8], F32)
    nhalf = sb("nhalf", [128, 4], F32)
    stats0 = sb("stats0", [128, TPB, 2, 6], F32)
    mv0 = sb("mv0", [128, TPB, 2], F32)
    rstd0 = sb("rstd0", [128, TPB], F32)
    nmean0 = sb("nmean0", [128, TPB], F32)
    nmr0 = sb("nmr0", [128, TPB], F32)
    stats1 = sb("stats1", [128, TPB, 2, 6], F32)
    mv1 = sb("mv1", [128, TPB, 2], F32)
    rstd1 = sb("rstd1", [128, TPB], F32)
    nmean1 = sb("nmean1", [128, TPB], F32)
    rden = sb("rden", [128, TPB], F32)

    NMM = 3
    NSC = 3
    mm = [ps("mm%d" % i, [128, 512], F32) for i in range(NMM)]
    mmb = [m[:, :].bitcast(BF16) for m in mm]
    sc = [ps("sc%d" % i, [128, 512], F32) for i in range(NSC)]
    pv = [ps("pv%d" % i, [128, TPB, 65], F32) for i in range(2)]

    P = Prog()
    mmc = [0]
    scc = [0]

    def next_mm():
        i = mmc[0] % NMM
        mmc[0] += 1
        return i

    def next_sc():
        i = scc[0] % NSC
        scc[0] += 1
        return i

    for dst, src, nm in ((g0T, g0T_d, "g0T"), (b0T, b0T_d, "b0T"), (ag0, g0bc_d, "ag0"),
                         (ab0, b0bc_d, "ab0"), (g1bc, g1bc_d, "g1bc"), (b1bc, b1bc_d, "b1bc"),
                         (cwh, cw_d, "cwh"), (cbh, cb_d, "cbh"), (nrc, rc_d, "nrc")):
        P.dma("sp", lambda e, dst=dst, src=src: e.dma_start(out=dst[:], in_=src), "c", writes=[nm])
    P.dma("sp", lambda e: e.dma_start(out=xs[:, 6:8, :].rearrange("p a d -> p (a d)"), in_=rtab_d),
          "c", writes=["xs6", "xs7"])
    P.fix_dma_key("c")

    P.op("pool", lambda e: e.memset(ident[:], 0.0), writes=["ident"])
    P.op("pool", lambda e: e.affine_select(out=ident[:], in_=ident[:], pattern=[[-1, 128]],
                                           compare_op=ALU.not_equal, fill=1.0, base=0,
                                           channel_multiplier=1),
         reads=["ident"], writes=["ident"])
    P.op("pool", lambda e: e.memset(nhalf[:], -0.5), writes=["nhalf"])
    vres = ["v%d_%d" % (s, t) for s in range(2) for t in range(TPB)]
    P.op("dve", lambda e: e.memset(vA[:], 1.0), writes=vres)
    P.op("dve", lambda e: e.memset(qz[:], 0.0), writes=["qz%d_%d" % (p, hh) for p in range(4) for hh in range(2)])
    P.op("dve", lambda e: e.tensor_scalar(out=ag0[:], in0=ag0[:], scalar1=ALPHA, scalar2=None, op0=ALU.mult),
         reads=["ag0"], writes=["ag0"])
    P.op("dve", lambda e: e.tensor_scalar(out=ab0[:], in0=ab0[:], scalar1=ALPHA, scalar2=None, op0=ALU.mult),
         reads=["ab0"], writes=["ab0"])
    P.op("dve", lambda e: e.tensor_scalar(out=cwh[:], in0=cwh[:], scalar1=0.5, scalar2=None, op0=ALU.mult),
         reads=["cwh"], writes=["cwh"])
    P.op("dve", lambda e: e.tensor_scalar(out=cbh[:], in0=cbh[:], scalar1=0.5, scalar2=None, op0=ALU.mult),
         reads=["cbh"], writes=["cbh"])
    P.op("dve", lambda e: e.tensor_scalar(out=nrc[:], in0=nrc[:], scalar1=-1.0, scalar2=None, op0=ALU.mult),
         reads=["nrc"], writes=["nrc"])
    rtab_sb = xs[:, 6:8, :].rearrange("p a d -> p (a d)")
    NEG = -240000.0
    for h in range(H):
        P.op("dve", lambda e, h=h: e.tensor_scalar(out=etab[:, h, :], in0=rtab_sb[:, h * 256:(h + 1) * 256],
                                                   scalar1=nrc[:, h:h + 1], scalar2=8.0, op0=ALU.add, op1=ALU.mult),
             reads=["xs6", "xs7", "nrc"], writes=["etab"])
    P.op("pool", lambda e: e.memset(etab[64:128, :, 0:64], NEG), reads=["etab"], writes=["etab"])
    P.op("pool", lambda e: e.memset(mk[:], 0.0), writes=["mk"])
    P.op("pool", lambda e: e.memset(mk[0:64, :], NEG), reads=["mk"], writes=["mk"])

    yT_f = yT[:, :, :].rearrange("p a b -> p (a b)").bitcast(F32)
    ytok_f = ytok[:, :, :].rearrange("p a b -> p (a b)").bitcast(F32)
    stg = [yT_f[:, 0:1024], yT_f[:, 1024:2048], ytok_f[:, 0:1024]]
    stg_res = [["yT%d" % c for c in range(4)], ["yT%d" % c for c in range(4, 8)], ["ytok"]]

    wchunks = []
    for g in range(4):
        for kc in range(8):
            wchunks.append((w_in_d[kc * 128:(kc + 1) * 128, g * 1024:(g + 1) * 1024],
                            w_in_sb[:, kc, g * 1024:(g + 1) * 1024], "w_in%d_%d" % (g, kc)))
    for kc in range(8):
        wchunks.append((w_out_d[kc * 128:(kc + 1) * 128, :], w_out_sb[:, kc, :], "w_out%d" % kc))
    NWC = len(wchunks)

    xhb_f = xhb[:, :, :].rearrange("p a b -> p (a b)").bitcast(F32)
    stg2 = [xhb_f[:, 0:1024], xhb_f[:, 1024:2048], xs[:, 7, :]]
    stg2_res = [["xhb0", "xhb1"], ["xhb2", "xhb3"], ["xs7"]]

    def w_stage(n):
        k = n % 3
        return (stg[k], stg_res[k], "ws%d" % k) if n < 16 else (stg2[k], stg2_res[k], "wt%d" % k)

    def w_dma(n):
        buf, res, key = w_stage(n)
        src = wchunks[n][0]
        P.dma("sp", lambda e, buf=buf, src=src: e.dma_start(out=buf, in_=src), key, writes=res)

    def w_cast(n):
        buf, bres, key = w_stage(n)
        dst, res = wchunks[n][1], wchunks[n][2]
        P.op("dve", lambda e, buf=buf, dst=dst: e.tensor_copy(out=dst, in_=buf), reads=bres, writes=[res])

    def w_item(n):
        if n + 2 < NWC:
            w_dma(n + 2)
        w_cast(n)

    def slot_of(b, t):
        return (b * TPB + t) % NSLOT

    def load_tile(b, t):
        n = b * TPB + t
        s = slot_of(b, t)
        P.dma("sp", lambda e, n=n, s=s: e.dma_start(out=xs[:, s, :], in_=x_d[n]),
              "xl%d" % s, writes=["xs%d" % s])

    def stage_load(b):
        for t in range(TPB):
            load_tile(b, t)

    def ln0_stats_k(b, t, k):
        s = slot_of(b, t)
        P.op("dve", lambda e, t=t, s=s, k=k: e.bn_stats(out=stats0[:, t, k, :],
                                                        in_=xs[:, s, k * 512:(k + 1) * 512]),
             reads=["xs%d" % s], writes=["stats0_%d" % t])

    def ln0_aggr(b, t):
        P.op("dve", lambda e, t=t: e.bn_aggr(out=mv0[:, t, :],
                                             in_=stats0[:, t, :, :].rearrange("p a b -> p (a b)")),
             reads=["stats0_%d" % t], writes=["mv0_%d" % t])

    def ln0_stats(b, t):
        ln0_stats_k(b, t, 0)
        ln0_stats_k(b, t, 1)
        ln0_aggr(b, t)

    def ln0_small(b, t):
        P.op("pool", lambda e, t=t: e.tensor_scalar(out=rstd0[:, t:t + 1], in0=mv0[:, t, 1:2], scalar1=LN_EPS,
                                                    scalar2=None, op0=ALU.add),
             reads=["mv0_%d" % t], writes=["rstd0_%d" % t])
        P.op("pool", lambda e, t=t: e.tensor_tensor(out=rstd0[:, t:t + 1], in0=rstd0[:, t:t + 1],
                                                    in1=nhalf[:, 0:1], op=ALU.pow),
             reads=["rstd0_%d" % t, "nhalf"], writes=["rstd0_%d" % t])
        P.op("pool", lambda e, t=t: e.tensor_scalar(out=nmean0[:, t:t + 1], in0=mv0[:, t, 0:1], scalar1=-1.0,
                                                    scalar2=None, op0=ALU.mult),
             reads=["mv0_%d" % t], writes=["nmean0_%d" % t])
        P.op("pool", lambda e, t=t: e.tensor_tensor(out=nmr0[:, t:t + 1], in0=nmean0[:, t:t + 1],
                                                    in1=rstd0[:, t:t + 1], op=ALU.mult),
             reads=["nmean0_%d" % t, "rstd0_%d" % t], writes=["nmr0_%d" % t])

    def ln0_act(b, t):
        s = slot_of(b, t)
        P.op("act", lambda e, t=t, s=s: e.activation(out=xhb[:, t, :], in_=xs[:, s, :], func=AF.Identity,
                                                     bias=nmr0[:, t:t + 1], scale=rstd0[:, t:t + 1]),
             reads=["xs%d" % s, "nmr0_%d" % t, "rstd0_%d" % t], writes=["xhb%d" % t])

    def ln0_res_a(b, t):
        s = slot_of(b, t)
        xr = ["xs%d" % s]
        P.op("dve", lambda e, t=t, s=s: e.scalar_tensor_tensor(
            out=xs[:, s, :], in0=xs[:, s, :], scalar=nmean0[:, t:t + 1], in1=ag0[:],
            op0=ALU.add, op1=ALU.mult),
            reads=xr + ["nmean0_%d" % t, "ag0"], writes=xr)

    def ln0_res_b(b, t):
        s = slot_of(b, t)
        xr = ["xs%d" % s]
        P.op("dve", lambda e, t=t, s=s: e.scalar_tensor_tensor(
            out=xs[:, s, :], in0=xs[:, s, :], scalar=rstd0[:, t:t + 1], in1=ab0[:],
            op0=ALU.mult, op1=ALU.add),
            reads=xr + ["rstd0_%d" % t, "ab0"], writes=xr)

    def ln0_res(b, t):
        ln0_res_a(b, t)
        ln0_res_b(b, t)

    def stage_ln0(b):
        for t in range(TPB):
            ln0_stats(b, t)
            ln0_small(b, t)
        for t in range(TPB):
            ln0_act(b, t)
        for t in range(TPB):
            ln0_res(b, t)

    def stage_T(b):
        for kp in range(4):
            i = next_mm()
            for kk in range(2):
                kc = 2 * kp + kk
                for t in range(TPB):
                    P.op("pe", lambda e, i=i, kc=kc, kk=kk, t=t: e.transpose(
                        mmb[i][:, kk * 512 + t * 128: kk * 512 + (t + 1) * 128],
                        xhb[:, t, kc * 128:(kc + 1) * 128], ident[:]),
                        reads=["xhb%d" % t, "ident"], writes=["mm%d" % i])
            for kk in range(2):
                kc = 2 * kp + kk
                P.op("act", lambda e, i=i, kc=kc, kk=kk: e.activation(
                    out=xnT[:, kc, :], in_=mmb[i][:, kk * 512:(kk + 1) * 512], func=AF.Identity,
                    bias=b0T[:, kc:kc + 1], scale=g0T[:, kc:kc + 1]),
                    reads=["mm%d" % i, "g0T", "b0T"], writes=["xnT%d" % kc])
            yield

    xnT_all = ["xnT%d" % kc for kc in range(8)]

    def proj_fm(cc):
        i = next_mm()
        for kc in range(8):
            P.op("pe", lambda e, i=i, kc=kc, cc=cc: e.matmul(
                mm[i][:, :], lhsT=w_in_sb[:, kc, cc * 128:(cc + 1) * 128], rhs=xnT[:, kc, :],
                start=(kc == 0), stop=(kc == 7)),
                reads=["w_in%d_%d" % (cc // 8, kc), "xnT%d" % kc], writes=["mm%d" % i])
            yield
        return i

    def proj_tm(t, c0):
        i = next_mm()
        for kc in range(8):
            P.op("pe", lambda e, i=i, kc=kc, t=t, c0=c0: e.matmul(
                mm[i][:, :], lhsT=xnT[:, kc, t * 128:(t + 1) * 128], rhs=w_in_sb[:, kc, c0:c0 + 512],
                start=(kc == 0), stop=(kc == 7)),
                reads=["w_in%d_%d" % (c0 // 1024, kc), "xnT%d" % kc], writes=["mm%d" % i])
            yield
        return i

    def drive(gen):
        for _ in gen:
            pass

    def interleave(streams):
        active = [[g, w] for g, w in streams]
        while active:
            for s in list(active):
                for _ in range(s[1]):
                    try:
                        next(s[0])
                    except StopIteration:
                        active.remove(s)
                        break

    def chain(gens, bg=None):
        n = len(gens)
        for k, g in enumerate(gens):
            yield from g
            if bg:
                take = -(-len(bg) // (n - k))
                for _ in range(take):
                    bg.pop(0)()

    def stage_P1(b, bg=None):
        sl = b % 2
        bg = bg or []
        for p in range(4):
            i = yield from proj_fm(p)
            P.op("act", lambda e, i=i, p=p: e.activation(
                out=qz[0:64, p, 0, :], in_=mm[i][0:64, :], func=AF.Copy),
                reads=["mm%d" % i], writes=["qz%d_0" % p])
            P.op("act", lambda e, i=i, p=p: e.activation(
                out=qz[64:128, p, 1, :], in_=mm[i][64:128, :], func=AF.Copy),
                reads=["mm%d" % i], writes=["qz%d_1" % p])
            if bg:
                bg.pop(0)()
            i = yield from proj_fm(4 + p)
            P.op("act", lambda e, i=i, p=p, sl=sl: e.activation(out=kT[:, sl, p, :], in_=mm[i][:, :], func=AF.Copy),
                 reads=["mm%d" % i], writes=["kT%d_%d" % (sl, p)])
            if bg:
                bg.pop(0)()

    def unit_v(b, t):
        sl = b % 2
        i = yield from proj_tm(t, 1024)
        P.op("act", lambda e, i=i, t=t, sl=sl: e.activation(
            out=vA[:, sl, t, :, 0:64], in_=mm[i][:, :].rearrange("p (h d) -> p h d", d=64),
            func=AF.Copy),
            reads=["mm%d" % i], writes=["v%d_%d" % (sl, t)])

    def unit_za(t):
        i = yield from proj_tm(t, 1536)
        P.op("act", lambda e, i=i: e.activation(out=tz[:], in_=mm[i][:, :], func=AF.Tanh, scale=0.5),
             reads=["mm%d" % i], writes=["tz"])
        P.op("dve", lambda e, i=i, t=t: e.scalar_tensor_tensor(
            out=gate[:, t, :], in0=tz[:], scalar=1.0, in1=mm[i][:, :], op0=ALU.add, op1=ALU.mult),
            reads=["tz", "mm%d" % i], writes=["gate%d" % t])

    tzb = [(tz[:], ["tz"]), (yT_f[:, 0:NTB], ["yT0", "yT1"])]

    def unit_C(cc, first):
        i = yield from proj_fm(20 + cc)
        P.op("act", lambda e, i=i: e.activation(out=ubuf[:, 2:NTB + 2], in_=mm[i][:, :], func=AF.Copy),
             reads=["mm%d" % i], writes=["u"])
        if first:
            P.op("dve", lambda e: e.memset(ubuf[:, 0:2], 0.0), writes=["uh"])
        else:
            P.op("dve", lambda e, cc=cc: e.tensor_copy(out=ubuf[:, 0:2], in_=halo[:, cc, :]),
                 reads=["halo%d" % cc], writes=["uh"])

    def unit_zc(cc):
        tzv, tzr = tzb[cc % 2]
        i = yield from proj_fm(28 + cc)
        P.op("act", lambda e, i=i, tzv=tzv: e.activation(out=tzv, in_=mm[i][:, :], func=AF.Tanh, scale=0.5),
             reads=["mm%d" % i], writes=tzr)
        P.op("dve", lambda e, i=i, tzv=tzv: e.scalar_tensor_tensor(
            out=tzv, in0=tzv, scalar=1.0, in1=mm[i][:, :], op0=ALU.add, op1=ALU.mult),
            reads=tzr + ["mm%d" % i], writes=tzr)

    def unit_B(cc):
        tzv, tzr = tzb[cc % 2]
        i = yield from proj_fm(16 + cc)
        P.op("dve", lambda e, i=i, tzv=tzv: e.tensor_tensor(out=tzv, in0=tzv, in1=mm[i][:, :], op=ALU.mult),
             reads=tzr + ["mm%d" % i], writes=tzr)

    def unit_h(cc):
        tzv, tzr = tzb[cc % 2]
        i = yield from proj_fm(24 + cc)
        P.op("dve", lambda e, i=i: e.tensor_tensor(out=ubuf[:, 2:NTB + 2], in0=ubuf[:, 2:NTB + 2],
                                                   in1=mm[i][:, :], op=ALU.mult),
             reads=["u", "mm%d" % i], writes=["u"])
        P.op("dve", lambda e, cc=cc: e.tensor_scalar(
            out=acc[:], in0=ubuf[:, 2:NTB + 2], scalar1=cwh[:, 8 + cc:9 + cc], scalar2=cbh[:, cc:cc + 1],
            op0=ALU.mult, op1=ALU.add),
            reads=["u", "cwh", "cbh"], writes=["acc"])
        P.op("dve", lambda e, cc=cc: e.scalar_tensor_tensor(
            out=acc[:], in0=ubuf[:, 1:NTB + 1], scalar=cwh[:, 4 + cc:5 + cc], in1=acc[:],
            op0=ALU.mult, op1=ALU.add),
            reads=["u", "uh", "cwh", "acc"], writes=["acc"])
        P.op("dve", lambda e, cc=cc: e.scalar_tensor_tensor(
            out=acc[:], in0=ubuf[:, 0:NTB], scalar=cwh[:, cc:cc + 1], in1=acc[:],
            op0=ALU.mult, op1=ALU.add),
            reads=["u", "uh", "cwh", "acc"], writes=["acc"])
        P.op("dve", lambda e, cc=cc: e.tensor_copy(out=halo[:, cc, :], in_=ubuf[:, NTB:NTB + 2]),
             reads=["u"], writes=["halo%d" % cc])
        P.op("dve", lambda e, cc=cc, tzv=tzv: e.tensor_tensor(out=yT[:, 4 + cc, :], in0=acc[:], in1=tzv,
                                                               op=ALU.mult),
             reads=["acc"] + tzr, writes=["yT%d" % (4 + cc)])

    def stage_QK(b, h, has_prev):
        sl = b % 2
        pair = h // 2
        hh = h % 2
        hb = h % 2
        groups = [(0, 1), (2,), (3,), (4,), (5,), (6, 7)] if has_prev else [(4,), (5,), (6, 7)]
        for grp in groups:
            i = next_sc()
            coff = 0
            for j in grp:
                n = NQ[j] * 64
                ks = sl if j >= 4 else 1 - sl
                kt = j % 4
                extra = []
                q0 = max(2 * j - 8, LO[j])
                q1 = min(2 * j - 5, HI[j])
                if q1 >= q0:
                    c0 = q0 - (2 * j - 8)
                    nn = (q1 - q0 + 1) * 64
                    extra.append((coff + (q0 - LO[j]) * 64, nn, ("etab", h, c0 * 64)))
                if j <= 3:
                    extra.append((coff + (HI[j] - LO[j]) * 64, 64, ("mk",)))
                P.op("pe", lambda e, i=i, n=n, coff=coff, ks=ks, kt=kt, pair=pair, hh=hh, j=j,
                     sp=(len(extra) == 0): e.matmul(
                    sc[i][:, coff:coff + n], lhsT=kT[:, ks, pair, kt * 128:(kt + 1) * 128],
                    rhs=qz[:, pair, hh, LO[j] * 64:(HI[j] + 1) * 64], start=True, stop=sp),
                    reads=["kT%d_%d" % (ks, pair), "qz%d_%d" % (pair, hh)], writes=["sc%d" % i])
                yield
                for xi, (a0, nn, src) in enumerate(extra):
                    last = xi == len(extra) - 1
                    if src[0] == "etab":
                        P.op("pe", lambda e, i=i, a0=a0, nn=nn, hq=src[1], c=src[2], last=last: e.matmul(
                            sc[i][:, a0:a0 + nn], lhsT=ident[:], rhs=etab[:, hq, c:c + nn], start=False, stop=last),
                            reads=["ident", "etab"], writes=["sc%d" % i])
                    else:
                        P.op("pe", lambda e, i=i, a0=a0, last=last: e.matmul(
                            sc[i][:, a0:a0 + 64], lhsT=ident[:], rhs=mk[:], start=False, stop=last),
                            reads=["ident", "mk"], writes=["sc%d" % i])
                    yield
                coff += n
            j0 = grp[0]
            P.op("act", lambda e, i=i, coff=coff, hb=hb, j0=j0: e.activation(
                out=PT[:, hb, OFF[j0]:OFF[j0] + coff], in_=sc[i][:, 0:coff], func=AF.Exp, scale=0.125),
                reads=["sc%d" % i], writes=["PT%d_%d" % (hb, j) for j in grp])

    def stage_PV(b, h, has_prev, pend):
        sl = b % 2
        hb = h % 2
        pvb = pv[hb]
        for t in range(TPB):
            items = [j for j in range(t, t + 5) if has_prev or j >= 4]
            for idx, j in enumerate(items):
                ks = sl if j >= 4 else 1 - sl
                a0 = OFF[j] + (2 * t - LO[j]) * 64
                P.op("pe", lambda e, j=j, ks=ks, a0=a0, t=t, h=h, hb=hb, pvb=pvb,
                     st=(idx == 0), sp=(idx == len(items) - 1): e.matmul(
                    pvb[:, t, 0:65], lhsT=PT[:, hb, a0:a0 + 128],
                    rhs=vA[:, ks, j % 4, h, 0:65], start=st, stop=sp),
                    reads=["PT%d_%d" % (hb, j), "v%d_%d" % (ks, j % 4)], writes=["pv%d" % hb])
                yield

        def evac():
            P.op("dve", lambda e, pvb=pvb: e.reciprocal(out=rden[:], in_=pvb[:, :, 64]),
                 reads=["pv%d" % hb], writes=["rden"])
            P.op("dve", lambda e, pvb=pvb: e.scalar_tensor_tensor(
                out=ynorm[:], in0=pvb[:, :, 0:64], scalar=0.5,
                in1=rden[:, :].unsqueeze(2).to_broadcast([128, TPB, 64]), op0=ALU.mult, op1=ALU.mult),
                reads=["pv%d" % hb, "rden"], writes=["ynorm"])
            P.op("dve", lambda e, h=h: e.tensor_tensor(
                out=ytok[:, :, h * 64:(h + 1) * 64], in0=ynorm[:], in1=gate[:, :, h * 64:(h + 1) * 64],
                op=ALU.mult),
                reads=["ynorm"] + ["gate%d" % t for t in range(TPB)], writes=["ytok"])
        if h == H - 2:
            evac()
        else:
            pend.append(evac)

    def stage_attn(b, has_prev, nxt):
        first = not has_prev
        units = [unit_v(b, t) for t in range(TPB)] + [unit_za(t) for t in range(TPB)]
        for cc in range(4):
            units += [unit_C(cc, first), unit_zc(cc), unit_B(cc), unit_h(cc)]
        per_head = [5, 3, 3, 3, 3, 3, 2, 2]
        ui = 0
        pend = []
        for h in range(H):
            for f in pend:
                f()
            pend = []
            bg = []
            if b == 0 and h < 3:
                bg += [lambda n=n: w_item(n) for n in range(16 + 8 * h, 24 + 8 * h)]
                if h == 2 and nxt is not None:
                    bg.append(lambda: load_tile(1, 3))
            if h < 4 and b >= 1:
                bg.append(lambda h=h: out_tile_a(b - 1, h))

                def fin(h=h):
                    out_tile_b(b - 1, h)
                    if nxt is not None:
                        load_tile(nxt, h)
                bg.append(fin)
            if nxt is not None:
                if 4 <= h < 8:
                    ln0_act(nxt, h - 4)
                if 2 <= h < 6:
                    bg.append(lambda h=h: ln0_stats_k(nxt, h - 2, 0))

                    def st2(h=h):
                        ln0_stats_k(nxt, h - 2, 1)
                        ln0_aggr(nxt, h - 2)
                        ln0_small(nxt, h - 2)
                    bg.append(st2)
            streams = [(stage_QK(b, h, has_prev), 2), (chain(units[ui:ui + per_head[h]], bg), 3)]
            ui += per_head[h]
            if h >= 1:
                streams.append((stage_PV(b, h - 1, has_prev, pend), 3))
            interleave(streams)
            for f in bg:
                f()
        for f in pend:
            f()
        pend = []
        assert ui == len(units)
        drive(stage_PV(b, H - 1, has_prev, pend))
        for f in pend:
            f()

    def stage_out(b, tgen):
        def tstep():
            if tgen is not None:
                next(tgen, None)
        tstep()
        tstep()
        for cp in range(2):
            i = next_mm()
            for kk in range(2):
                c = 2 * cp + kk
                for t in range(TPB):
                    P.op("pe", lambda e, i=i, c=c, kk=kk, t=t: e.transpose(
                        mmb[i][:, kk * 512 + t * 128: kk * 512 + (t + 1) * 128],
                        ytok[:, t, c * 128:(c + 1) * 128], ident[:]),
                        reads=["ytok", "ident"], writes=["mm%d" % i])
            for kk in range(2):
                c = 2 * cp + kk
                P.op("dve", lambda e, i=i, c=c, kk=kk: e.tensor_copy(
                    out=yT[:, c, :], in_=mmb[i][:, kk * 512:(kk + 1) * 512]),
                    reads=["mm%d" % i], writes=["yT%d" % c])
        for t in range(TPB):
            s = slot_of(b, t)
            xr = ["xs%d" % s]
            if t in (1, 2):
                tstep()
            for hf in range(2):
                i = next_mm()
                for c in range(8):
                    P.op("pe", lambda e, i=i, c=c, t=t, hf=hf: e.matmul(
                        mm[i][:, :], lhsT=yT[:, c, t * 128:(t + 1) * 128],
                        rhs=w_out_sb[:, c, hf * 512:(hf + 1) * 512], start=(c == 0), stop=(c == 7)),
                        reads=["w_out%d" % c, "yT%d" % c], writes=["mm%d" % i])
                P.op("dve", lambda e, i=i, s=s, hf=hf: e.tensor_tensor(
                    out=xs[:, s, hf * 512:(hf + 1) * 512], in0=xs[:, s, hf * 512:(hf + 1) * 512],
                    in1=mm[i][:, :], op=ALU.add),
                    reads=xr + ["mm%d" % i], writes=xr)
                P.op("dve", lambda e, t=t, s=s, hf=hf: e.bn_stats(out=stats1[:, t, hf, :],
                                                                  in_=xs[:, s, hf * 512:(hf + 1) * 512]),
                     reads=xr, writes=["stats1_%d" % t])
            P.op("dve", lambda e, t=t: e.bn_aggr(out=mv1[:, t, :],
                                                 in_=stats1[:, t, :, :].rearrange("p a b -> p (a b)")),
                 reads=["stats1_%d" % t], writes=["mv1"])

    def stage_out_pool(b):
        P.op("pool", lambda e: e.tensor_scalar(out=rstd1[:], in0=mv1[:, :, 1], scalar1=LN_EPS, scalar2=None,
                                               op0=ALU.add),
             reads=["mv1"], writes=["rstd1"])
        P.op("pool", lambda e: e.tensor_tensor(out=rstd1[:], in0=rstd1[:], in1=nhalf[:], op=ALU.pow),
             reads=["rstd1", "nhalf"], writes=["rstd1"])
        P.op("pool", lambda e: e.tensor_scalar(out=nmean1[:], in0=mv1[:, :, 0], scalar1=-1.0, scalar2=None,
                                               op0=ALU.mult),
             reads=["mv1"], writes=["nmean1"])

    def out_tile_a(b, t):
        s = slot_of(b, t)
        xr = ["xs%d" % s]
        P.op("dve", lambda e, t=t, s=s: e.scalar_tensor_tensor(
            out=xs[:, s, :], in0=xs[:, s, :], scalar=nmean1[:, t:t + 1], in1=g1bc[:],
            op0=ALU.add, op1=ALU.mult),
            reads=xr + ["nmean1", "g1bc"], writes=xr)

    def out_tile(b, t):
        out_tile_a(b, t)
        out_tile_b(b, t)

    def out_tile_b(b, t):
        s = slot_of(b, t)
        n = b * TPB + t
        xr = ["xs%d" % s]
        P.op("dve", lambda e, t=t, s=s: e.scalar_tensor_tensor(
            out=xs[:, s, :], in0=xs[:, s, :], scalar=rstd1[:, t:t + 1], in1=b1bc[:],
            op0=ALU.mult, op1=ALU.add),
            reads=xr + ["rstd1", "b1bc"], writes=xr)
        P.dma("sp", lambda e, n=n, s=s: e.dma_start(out=out_d[n], in_=xs[:, s, :]),
              "st%d" % s, reads=xr)

    stage_load(0)
    w_dma(0)
    w_dma(1)
    stage_ln0(0)
    drive(stage_T(0))
    for n in range(8):
        w_item(n)
    drive(stage_P1(0))
    for n in range(8, 16):
        w_item(n)
    if nblk > 1:
        for t in range(3):
            load_tile(1, t)
    for b in range(nblk):
        has_prev = (b % nbs) != 0
        nxt = b + 1 if b + 1 < nblk else None
        stage_attn(b, has_prev, nxt)
        stage_out(b, stage_T(nxt) if nxt is not None else None)
        stage_out_pool(b)
        if nxt is not None:
            rb = []
            for t in range(TPB):
                rb += [lambda t=t: ln0_res_a(nxt, t), lambda t=t: ln0_res_b(nxt, t)]
            drive(stage_P1(nxt, rb))
            for f in rb:
                f()
    for t in range(TPB):
        out_tile(nblk - 1, t)
    P.wait_all("sp", ["xs%d" % s for s in range(NSLOT)])

    stats = P.emit(nc, es)
    stats["sbuf_free"] = nc.sbuf_bytes_remaining
    stats["prog"] = P
    es.close()
    return nc, stats


def _host_layout(ln0_g, ln0_b, w_in, rel_bias, conv_w, conv_b, w_out, ln_g, ln_b):
    f = np.float32
    kp = np.arange(128)[:, None, None]
    c = np.arange(4)[None, :, None]
    i = np.arange(64)[None, None, :]
    dist = (c - kp // 64) * 64 + i - (kp % 64)
    idx = np.clip(dist, -128, 128) + 128
    rb = np.asarray(rel_bias[0], f)
    rtab = np.ascontiguousarray(rb[:, idx].transpose(1, 0, 2, 3)).reshape(128, 2048)
    rc = np.ascontiguousarray(np.broadcast_to(rb[:, 256][None, :], (128, 8)))
    cw = np.asarray(conv_w[0], f)
    cwl = np.ascontiguousarray(cw.reshape(3, 4, 128).transpose(2, 0, 1)).reshape(128, 12)
    cbl = np.ascontiguousarray(np.asarray(conv_b[0], f).reshape(4, 128).T)
    bc = lambda v: np.ascontiguousarray(np.broadcast_to(np.asarray(v, f).reshape(1, D), (128, D)))
    return {
        "w_in": np.ascontiguousarray(np.asarray(w_in[0], f)),
        "w_out": np.ascontiguousarray(np.asarray(w_out[0], f)),
        "g0T": np.ascontiguousarray(np.asarray(ln0_g, f).reshape(8, 128).T),
        "b0T": np.ascontiguousarray(np.asarray(ln0_b, f).reshape(8, 128).T),
        "g0bc": bc(ln0_g), "b0bc": bc(ln0_b), "g1bc": bc(ln_g[0]), "b1bc": bc(ln_b[0]),
        "cw": cwl, "cb": cbl, "rtab": rtab, "rc": rc,
    }


def kernel(x, ln0_g, ln0_b, w_in, rel_bias, conv_w, conv_b, w_out, ln_g, ln_b):
    x = np.asarray(x, np.float32)
    bsz, seqlen, d = x.shape
    nseq = bsz // N_CORES
    nc, _ = build(nseq, seqlen)
    shared = _host_layout(ln0_g, ln0_b, w_in, rel_bias, conv_w, conv_b, w_out, ln_g, ln_b)
    in_maps = []
    for c in range(N_CORES):
        m = dict(shared)
        m["x"] = np.ascontiguousarray(x[c * nseq:(c + 1) * nseq].reshape(nseq * seqlen, d))
        in_maps.append(m)
    res = run_bass_kernel_spmd(nc, in_maps, core_ids=list(range(N_CORES)))
    outs = [np.asarray(r["out"], np.float32).reshape(nseq, seqlen, d) for r in res.results]
    return np.concatenate(outs, axis=0)
```

```python
import numpy as np
from contextlib import ExitStack

import concourse.bass as bass
import concourse.mybir as mybir
from concourse.bass_utils import run_bass_kernel_spmd

F32 = mybir.dt.float32
BF16 = mybir.dt.bfloat16
ALU = mybir.AluOpType
AF = mybir.ActivationFunctionType

N_CORES = 8
D = 1024
NCOLS = 4096
H = 8
DH = 64
NTB = 512
TPB = 4
LN_EPS = 1e-5
ALPHA = 2.0 ** 0.25
NSLOT = 8
ENGS = ("pe", "act", "dve", "pool", "sp")

LO = [max(0, 2 * j - 8) for j in range(8)]
HI = [min(7, 2 * j + 1) for j in range(8)]
NQ = [HI[j] - LO[j] + 1 for j in range(8)]
OFF = [64 * sum(NQ[:j]) for j in range(8)]
PTW = 64 * sum(NQ)


class Prog:
    def __init__(self):
        self.q = {e: [] for e in ENGS}
        self.res = {}
        self.dma_cnt = {}

    def _deps(self, reads, writes):
        deps = []
        for r in reads:
            st = self.res.get(r)
            if st and st[0] is not None:
                deps.append(("raw", st[0]))
        for w in writes:
            st = self.res.get(w)
            if st:
                if st[0] is not None:
                    deps.append(("waw", st[0]))
                for ev in st[1].values():
                    deps.append(("war", ev))
        return deps

    def _commit(self, ev, reads, writes):
        for r in reads:
            st = self.res.setdefault(r, [None, {}])
            st[1][ev[1]] = ev
        for w in writes:
            self.res[w] = [ev, {}]

    def op(self, eng, fn, reads=(), writes=()):
        idx = len(self.q[eng])
        self.q[eng].append(dict(fn=fn, deps=self._deps(reads, writes), mark=False, dma=None,
                                tag="%s<-%s" % (",".join(writes), ",".join(reads))))
        self._commit(("e", eng, idx), reads, writes)

    def dma(self, eng, fn, key, reads=(), writes=()):
        val = self.dma_cnt.get(key, 0) + 16
        self.dma_cnt[key] = val
        self.q[eng].append(dict(fn=fn, deps=self._deps(reads, writes), mark=False, dma=(key, val)))
        self._commit(("d", key, val), reads, writes)

    def fix_dma_key(self, key):
        tot = self.dma_cnt[key]
        for st in self.res.values():
            if st[0] is not None and st[0][0] == "d" and st[0][1] == key:
                st[0] = ("d", key, tot)

    def wait_all(self, eng, names):
        deps = []
        for n in names:
            st = self.res.get(n)
            if st:
                if st[0] is not None:
                    deps.append(("raw", st[0]))
                for ev in st[1].values():
                    deps.append(("war", ev))
        self.q[eng].append(dict(fn=None, deps=deps, mark=False, dma=None))

    @staticmethod
    def _skip(kind, ev, e):
        return ev[0] == "e" and ev[1] == e and e == "pe"

    def emit(self, nc, es):
        for e in ENGS:
            for ins in self.q[e]:
                for kind, ev in ins["deps"]:
                    if ev[0] == "e" and not self._skip(kind, ev, e):
                        self.q[ev[1]][ev[2]]["mark"] = True
        for e in ENGS:
            c = 0
            for ins in self.q[e]:
                if ins["mark"]:
                    c += 1
                    ins["cnt"] = c
        esem = {e: es.enter_context(nc.semaphore("s_" + e)) for e in ENGS}
        dsem = {k: es.enter_context(nc.semaphore("d_" + k)) for k in self.dma_cnt}
        stats = {e: [len(self.q[e]), 0] for e in ENGS}

        def run(e, eng):
            seen = {}
            for ins in self.q[e]:
                need = {}
                for kind, ev in ins["deps"]:
                    if self._skip(kind, ev, e):
                        continue
                    if ev[0] == "e":
                        k = ("e", ev[1])
                        v = self.q[ev[1]][ev[2]]["cnt"]
                    else:
                        k = ("d", ev[1])
                        v = ev[2]
                    if v > need.get(k, 0):
                        need[k] = v
                for k, v in need.items():
                    if seen.get(k, 0) >= v:
                        continue
                    seen[k] = v
                    eng.wait_ge(esem[k[1]] if k[0] == "e" else dsem[k[1]], v)
                    stats[e][1] += 1
                if ins["fn"] is None:
                    continue
                r = ins["fn"](eng)
                if ins["dma"] is not None:
                    r.then_inc(dsem[ins["dma"][0]], 16)
                elif ins["mark"]:
                    r.then_inc(esem[e], 1)

        block = es.enter_context(nc.Block())

        @block.tensor
        def _(eng):
            run("pe", eng)

        @block.scalar
        def _(eng):
            run("act", eng)

        @block.vector
        def _(eng):
            run("dve", eng)

        @block.gpsimd
        def _(eng):
            run("pool", eng)

        @block.sync
        def _(eng):
            run("sp", eng)

        return stats


def build(nseq, seqlen):
    ntok = nseq * seqlen
    nbs = seqlen // NTB
    nblk = nseq * nbs
    ntile = ntok // 128

    nc = bass.Bass("TRN2", target_bir_lowering=False)
    es = ExitStack()

    def din(name, shape):
        return nc.dram_tensor(name, shape, F32, kind="ExternalInput").ap()

    x_d = din("x", [ntok, D]).rearrange("(n p) d -> n p d", p=128)
    w_in_d = din("w_in", [D, NCOLS])
    w_out_d = din("w_out", [D, D])
    g0T_d = din("g0T", [128, 8])
    b0T_d = din("b0T", [128, 8])
    g0bc_d = din("g0bc", [128, D])
    b0bc_d = din("b0bc", [128, D])
    g1bc_d = din("g1bc", [128, D])
    b1bc_d = din("b1bc", [128, D])
    cw_d = din("cw", [128, 12])
    cb_d = din("cb", [128, 4])
    rtab_d = din("rtab", [128, 2048])
    rc_d = din("rc", [128, 8])
    out_d = nc.dram_tensor("out", [ntok, D], F32, kind="ExternalOutput").ap().rearrange(
        "(n p) d -> n p d", p=128)

    def sb(name, shape, dt):
        return es.enter_context(nc.sbuf_tensor(name, shape, dt))

    def ps(name, shape, dt):
        return es.enter_context(nc.psum_tensor(name, shape, dt))

    w_in_sb = sb("w_in_sb", [128, 8, NCOLS], BF16)
    w_out_sb = sb("w_out_sb", [128, 8, D], BF16)
    ag0 = sb("ag0", [128, D], F32)
    ab0 = sb("ab0", [128, D], F32)
    g1bc = sb("g1bc_sb", [128, D], F32)
    b1bc = sb("b1bc_sb", [128, D], F32)
    xs = sb("xs", [128, NSLOT, D], F32)
    xhb = sb("xhb", [128, TPB, D], BF16)
    xnT = sb("xnT", [128, 8, NTB], BF16)
    qz = sb("qz", [128, 4, 2, NTB], BF16)
    kT = sb("kT", [128, 2, 4, NTB], BF16)
    vA = sb("vA", [128, 2, TPB, H, 65], BF16)
    PT = sb("PT", [128, 2, PTW], BF16)
    etab = sb("etab", [128, H, 256], BF16)
    gate = sb("gate", [128, TPB, 512], BF16)
    ytok = sb("ytok", [128, TPB, 512], BF16)
    ynorm = sb("ynorm", [128, TPB, 64], F32)
    yT = sb("yT", [128, 8, NTB], BF16)
    ubuf = sb("ubuf", [128, NTB + 2], F32)
    acc = sb("acc", [128, NTB], F32)
    tz = sb("tz", [128, NTB], F32)
    halo = sb("halo", [128, 4, 2], F32)
    ident = sb("ident", [128, 128], BF16)
    mk = sb("mk", [128, 64], BF16)
    g0T = sb("g0T_sb", [128, 8], F32)
    b0T = sb("b0T_sb", [128, 8], F32)
    cwh = sb("cwh", [128, 12], F32)
    cbh = sb("cbh", [128, 4], F32)
    nrc = sb("nrc", [128, 8], F32)
    nhalf = sb("nhalf", [128, 4], F32)
    stats0 = sb("stats0", [128, TPB, 2, 6], F32)
    mv0 = sb("mv0", [128, TPB, 2], F32)
    rstd0 = sb("rstd0", [128, TPB], F32)
    nmean0 = sb("nmean0", [128, TPB], F32)
    nmr0 = sb("nmr0", [128, TPB], F32)
    stats1 = sb("stats1", [128, TPB, 2, 6], F32)
    mv1 = sb("mv1", [128, TPB, 2], F32)
    rstd1 = sb("rstd1", [128, TPB], F32)
    nmean1 = sb("nmean1", [128, TPB], F32)
    rden = sb("rden", [128, TPB], F32)

    NMM = 3
    NSC = 3
    mm = [ps("mm%d" % i, [128, 512], F32) for i in range(NMM)]
    mmb = [m[:, :].bitcast(BF16) for m in mm]
    sc = [ps("sc%d" % i, [128, 512], F32) for i in range(NSC)]
    pv = [ps("pv%d" % i, [128, TPB, 65], F32) for i in range(2)]

    P = Prog()
    mmc = [0]
    scc = [0]

    def next_mm():
        i = mmc[0] % NMM
        mmc[0] += 1
        return i

    def next_sc():
        i = scc[0] % NSC
        scc[0] += 1
        return i

    for dst, src, nm in ((g0T, g0T_d, "g0T"), (b0T, b0T_d, "b0T"), (ag0, g0bc_d, "ag0"),
                         (ab0, b0bc_d, "ab0"), (g1bc, g1bc_d, "g1bc"), (b1bc, b1bc_d, "b1bc"),
                         (cwh, cw_d, "cwh"), (cbh, cb_d, "cbh"), (nrc, rc_d, "nrc")):
        P.dma("sp", lambda e, dst=dst, src=src: e.dma_start(out=dst[:], in_=src), "c", writes=[nm])
    P.dma("sp", lambda e: e.dma_start(out=xs[:, 6:8, :].rearrange("p a d -> p (a d)"), in_=rtab_d),
          "c", writes=["xs6", "xs7"])
    P.fix_dma_key("c")

    P.op("pool", lambda e: e.memset(ident[:], 0.0), writes=["ident"])
    P.op("pool", lambda e: e.affine_select(out=ident[:], in_=ident[:], pattern=[[-1, 128]],
                                           compare_op=ALU.not_equal, fill=1.0, base=0,
                                           channel_multiplier=1),
         reads=["ident"], writes=["ident"])
    P.op("pool", lambda e: e.memset(nhalf[:], -0.5), writes=["nhalf"])
    vres = ["v%d_%d" % (s, t) for s in range(2) for t in range(TPB)]
    P.op("dve", lambda e: e.memset(vA[:], 1.0), writes=vres)
    P.op("dve", lambda e: e.memset(qz[:], 0.0), writes=["qz%d_%d" % (p, hh) for p in range(4) for hh in range(2)])
    P.op("dve", lambda e: e.tensor_scalar(out=ag0[:], in0=ag0[:], scalar1=ALPHA, scalar2=None, op0=ALU.mult),
         reads=["ag0"], writes=["ag0"])
    P.op("dve", lambda e: e.tensor_scalar(out=ab0[:], in0=ab0[:], scalar1=ALPHA, scalar2=None, op0=ALU.mult),
         reads=["ab0"], writes=["ab0"])
    P.op("dve", lambda e: e.tensor_scalar(out=cwh[:], in0=cwh[:], scalar1=0.5, scalar2=None, op0=ALU.mult),
         reads=["cwh"], writes=["cwh"])
    P.op("dve", lambda e: e.tensor_scalar(out=cbh[:], in0=cbh[:], scalar1=0.5, scalar2=None, op0=ALU.mult),
         reads=["cbh"], writes=["cbh"])
    P.op("dve", lambda e: e.tensor_scalar(out=nrc[:], in0=nrc[:], scalar1=-1.0, scalar2=None, op0=ALU.mult),
         reads=["nrc"], writes=["nrc"])
    rtab_sb = xs[:, 6:8, :].rearrange("p a d -> p (a d)")
    NEG = -240000.0
    for h in range(H):
        P.op("dve", lambda e, h=h: e.tensor_scalar(out=etab[:, h, :], in0=rtab_sb[:, h * 256:(h + 1) * 256],
                                                   scalar1=nrc[:, h:h + 1], scalar2=8.0, op0=ALU.add, op1=ALU.mult),
             reads=["xs6", "xs7", "nrc"], writes=["etab"])
    P.op("pool", lambda e: e.memset(etab[64:128, :, 0:64], NEG), reads=["etab"], writes=["etab"])
    P.op("pool", lambda e: e.memset(mk[:], 0.0), writes=["mk"])
    P.op("pool", lambda e: e.memset(mk[0:64, :], NEG), reads=["mk"], writes=["mk"])

    yT_f = yT[:, :, :].rearrange("p a b -> p (a b)").bitcast(F32)
    ytok_f = ytok[:, :, :].rearrange("p a b -> p (a b)").bitcast(F32)
    stg = [yT_f[:, 0:1024], yT_f[:, 1024:2048], ytok_f[:, 0:1024]]
    stg_res = [["yT%d" % c for c in range(4)], ["yT%d" % c for c in range(4, 8)], ["ytok"]]

    wchunks = []
    for g in range(4):
        for kc in range(8):
            wchunks.append((w_in_d[kc * 128:(kc + 1) * 128, g * 1024:(g + 1) * 1024],
                            w_in_sb[:, kc, g * 1024:(g + 1) * 1024], "w_in%d_%d" % (g, kc)))
    for kc in range(8):
        wchunks.append((w_out_d[kc * 128:(kc + 1) * 128, :], w_out_sb[:, kc, :], "w_out%d" % kc))
    NWC = len(wchunks)

    xhb_f = xhb[:, :, :].rearrange("p a b -> p (a b)").bitcast(F32)
    stg2 = [xhb_f[:, 0:1024], xhb_f[:, 1024:2048], xs[:, 7, :]]
    stg2_res = [["xhb0", "xhb1"], ["xhb2", "xhb3"], ["xs7"]]

    def w_stage(n):
        k = n % 3
        return (stg[k], stg_res[k], "ws%d" % k) if n < 16 else (stg2[k], stg2_res[k], "wt%d" % k)

    def w_dma(n):
        buf, res, key = w_stage(n)
        src = wchunks[n][0]
        P.dma("sp", lambda e, buf=buf, src=src: e.dma_start(out=buf, in_=src), key, writes=res)

    def w_cast(n):
        buf, bres, key = w_stage(n)
        dst, res = wchunks[n][1], wchunks[n][2]
        P.op("dve", lambda e, buf=buf, dst=dst: e.tensor_copy(out=dst, in_=buf), reads=bres, writes=[res])

    def w_item(n):
        if n + 2 < NWC:
            w_dma(n + 2)
        w_cast(n)

    def slot_of(b, t):
        return (b * TPB + t) % NSLOT

    def load_tile(b, t):
        n = b * TPB + t
        s = slot_of(b, t)
        P.dma("sp", lambda e, n=n, s=s: e.dma_start(out=xs[:, s, :], in_=x_d[n]),
              "xl%d" % s, writes=["xs%d" % s])

    def stage_load(b):
        for t in range(TPB):
            load_tile(b, t)

    def ln0_stats_k(b, t, k):
        s = slot_of(b, t)
        P.op("dve", lambda e, t=t, s=s, k=k: e.bn_stats(out=stats0[:, t, k, :],
                                                        in_=xs[:, s, k * 512:(k + 1) * 512]),
             reads=["xs%d" % s], writes=["stats0_%d" % t])

    def ln0_aggr(b, t):
        P.op("dve", lambda e, t=t: e.bn_aggr(out=mv0[:, t, :],
                                             in_=stats0[:, t, :, :].rearrange("p a b -> p (a b)")),
             reads=["stats0_%d" % t], writes=["mv0_%d" % t])

    def ln0_stats(b, t):
        ln0_stats_k(b, t, 0)
        ln0_stats_k(b, t, 1)
        ln0_aggr(b, t)

    def ln0_small(b, t):
        P.op("pool", lambda e, t=t: e.tensor_scalar(out=rstd0[:, t:t + 1], in0=mv0[:, t, 1:2], scalar1=LN_EPS,
                                                    scalar2=None, op0=ALU.add),
             reads=["mv0_%d" % t], writes=["rstd0_%d" % t])
        P.op("pool", lambda e, t=t: e.tensor_tensor(out=rstd0[:, t:t + 1], in0=rstd0[:, t:t + 1],
                                                    in1=nhalf[:, 0:1], op=ALU.pow),
             reads=["rstd0_%d" % t, "nhalf"], writes=["rstd0_%d" % t])
        P.op("pool", lambda e, t=t: e.tensor_scalar(out=nmean0[:, t:t + 1], in0=mv0[:, t, 0:1], scalar1=-1.0,
                                                    scalar2=None, op0=ALU.mult),
             reads=["mv0_%d" % t], writes=["nmean0_%d" % t])
        P.op("pool", lambda e, t=t: e.tensor_tensor(out=nmr0[:, t:t + 1], in0=nmean0[:, t:t + 1],
                                                    in1=rstd0[:, t:t + 1], op=ALU.mult),
             reads=["nmean0_%d" % t, "rstd0_%d" % t], writes=["nmr0_%d" % t])

    def ln0_act(b, t):
        s = slot_of(b, t)
        P.op("act", lambda e, t=t, s=s: e.activation(out=xhb[:, t, :], in_=xs[:, s, :], func=AF.Identity,
                                                     bias=nmr0[:, t:t + 1], scale=rstd0[:, t:t + 1]),
             reads=["xs%d" % s, "nmr0_%d" % t, "rstd0_%d" % t], writes=["xhb%d" % t])

    def ln0_res_a(b, t):
        s = slot_of(b, t)
        xr = ["xs%d" % s]
        P.op("dve", lambda e, t=t, s=s: e.scalar_tensor_tensor(
            out=xs[:, s, :], in0=xs[:, s, :], scalar=nmean0[:, t:t + 1], in1=ag0[:],
            op0=ALU.add, op1=ALU.mult),
            reads=xr + ["nmean0_%d" % t, "ag0"], writes=xr)

    def ln0_res_b(b, t):
        s = slot_of(b, t)
        xr = ["xs%d" % s]
        P.op("dve", lambda e, t=t, s=s: e.scalar_tensor_tensor(
            out=xs[:, s, :], in0=xs[:, s, :], scalar=rstd0[:, t:t + 1], in1=ab0[:],
            op0=ALU.mult, op1=ALU.add),
            reads=xr + ["rstd0_%d" % t, "ab0"], writes=xr)

    def ln0_res(b, t):
        ln0_res_a(b, t)
        ln0_res_b(b, t)

    def stage_ln0(b):
        for t in range(TPB):
            ln0_stats(b, t)
            ln0_small(b, t)
        for t in range(TPB):
            ln0_act(b, t)
        for t in range(TPB):
            ln0_res(b, t)

    def stage_T(b):
        for kp in range(4):
            i = next_mm()
            for kk in range(2):
                kc = 2 * kp + kk
                for t in range(TPB):
                    P.op("pe", lambda e, i=i, kc=kc, kk=kk, t=t: e.transpose(
                        mmb[i][:, kk * 512 + t * 128: kk * 512 + (t + 1) * 128],
                        xhb[:, t, kc * 128:(kc + 1) * 128], ident[:]),
                        reads=["xhb%d" % t, "ident"], writes=["mm%d" % i])
            for kk in range(2):
                kc = 2 * kp + kk
                P.op("act", lambda e, i=i, kc=kc, kk=kk: e.activation(
                    out=xnT[:, kc, :], in_=mmb[i][:, kk * 512:(kk + 1) * 512], func=AF.Identity,
                    bias=b0T[:, kc:kc + 1], scale=g0T[:, kc:kc + 1]),
                    reads=["mm%d" % i, "g0T", "b0T"], writes=["xnT%d" % kc])
            yield

    xnT_all = ["xnT%d" % kc for kc in range(8)]

    def proj_fm(cc):
        i = next_mm()
        for kc in range(8):
            P.op("pe", lambda e, i=i, kc=kc, cc=cc: e.matmul(
                mm[i][:, :], lhsT=w_in_sb[:, kc, cc * 128:(cc + 1) * 128], rhs=xnT[:, kc, :],
                start=(kc == 0), stop=(kc == 7)),
                reads=["w_in%d_%d" % (cc // 8, kc), "xnT%d" % kc], writes=["mm%d" % i])
            yield
        return i

    def proj_tm(t, c0):
        i = next_mm()
        for kc in range(8):
            P.op("pe", lambda e, i=i, kc=kc, t=t, c0=c0: e.matmul(
                mm[i][:, :], lhsT=xnT[:, kc, t * 128:(t + 1) * 128], rhs=w_in_sb[:, kc, c0:c0 + 512],
                start=(kc == 0), stop=(kc == 7)),
                reads=["w_in%d_%d" % (c0 // 1024, kc), "xnT%d" % kc], writes=["mm%d" % i])
            yield
        return i

    def drive(gen):
        for _ in gen:
            pass

    def interleave(streams):
        active = [[g, w] for g, w in streams]
        while active:
            for s in list(active):
                for _ in range(s[1]):
                    try:
                        next(s[0])
                    except StopIteration:
                        active.remove(s)
                        break

    def chain(gens, bg=None):
        n = len(gens)
        for k, g in enumerate(gens):
            yield from g
            if bg:
                take = -(-len(bg) // (n - k))
                for _ in range(take):
                    bg.pop(0)()

    def stage_P1(b, bg=None):
        sl = b % 2
        bg = bg or []
        for p in range(4):
            i = yield from proj_fm(p)
            P.op("act", lambda e, i=i, p=p: e.activation(
                out=qz[0:64, p, 0, :], in_=mm[i][0:64, :], func=AF.Copy),
                reads=["mm%d" % i], writes=["qz%d_0" % p])
            P.op("act", lambda e, i=i, p=p: e.activation(
                out=qz[64:128, p, 1, :], in_=mm[i][64:128, :], func=AF.Copy),
                reads=["mm%d" % i], writes=["qz%d_1" % p])
            if bg:
                bg.pop(0)()
            i = yield from proj_fm(4 + p)
            P.op("act", lambda e, i=i, p=p, sl=sl: e.activation(out=kT[:, sl, p, :], in_=mm[i][:, :], func=AF.Copy),
                 reads=["mm%d" % i], writes=["kT%d_%d" % (sl, p)])
            if bg:
                bg.pop(0)()

    def unit_v(b, t):
        sl = b % 2
        i = yield from proj_tm(t, 1024)
        P.op("act", lambda e, i=i, t=t, sl=sl: e.activation(
            out=vA[:, sl, t, :, 0:64], in_=mm[i][:, :].rearrange("p (h d) -> p h d", d=64),
            func=AF.Copy),
            reads=["mm%d" % i], writes=["v%d_%d" % (sl, t)])

    def unit_za(t):
        i = yield from proj_tm(t, 1536)
        P.op("act", lambda e, i=i: e.activation(out=tz[:], in_=mm[i][:, :], func=AF.Tanh, scale=0.5),
             reads=["mm%d" % i], writes=["tz"])
        P.op("dve", lambda e, i=i, t=t: e.scalar_tensor_tensor(
            out=gate[:, t, :], in0=tz[:], scalar=1.0, in1=mm[i][:, :], op0=ALU.add, op1=ALU.mult),
            reads=["tz", "mm%d" % i], writes=["gate%d" % t])

    tzb = [(tz[:], ["tz"]), (yT_f[:, 0:NTB], ["yT0", "yT1"])]

    def unit_C(cc, first):
        i = yield from proj_fm(20 + cc)
        P.op("act", lambda e, i=i: e.activation(out=ubuf[:, 2:NTB + 2], in_=mm[i][:, :], func=AF.Copy),
             reads=["mm%d" % i], writes=["u"])
        if first:
            P.op("dve", lambda e: e.memset(ubuf[:, 0:2], 0.0), writes=["uh"])
        else:
            P.op("dve", lambda e, cc=cc: e.tensor_copy(out=ubuf[:, 0:2], in_=halo[:, cc, :]),
                 reads=["halo%d" % cc], writes=["uh"])

    def unit_zc(cc):
        tzv, tzr = tzb[cc % 2]
        i = yield from proj_fm(28 + cc)
        P.op("act", lambda e, i=i, tzv=tzv: e.activation(out=tzv, in_=mm[i][:, :], func=AF.Tanh, scale=0.5),
             reads=["mm%d" % i], writes=tzr)
        P.op("dve", lambda e, i=i, tzv=tzv: e.scalar_tensor_tensor(
            out=tzv, in0=tzv, scalar=1.0, in1=mm[i][:, :], op0=ALU.add, op1=ALU.mult),
            reads=tzr + ["mm%d" % i], writes=tzr)

    def unit_B(cc):
        tzv, tzr = tzb[cc % 2]
        i = yield from proj_fm(16 + cc)
        P.op("dve", lambda e, i=i, tzv=tzv: e.tensor_tensor(out=tzv, in0=tzv, in1=mm[i][:, :], op=ALU.mult),
             reads=tzr + ["mm%d" % i], writes=tzr)

    def unit_h(cc, late=None):
        tzv, tzr = tzb[cc % 2]
        i = yield from proj_fm(24 + cc)
        P.op("dve", lambda e, i=i: e.tensor_tensor(out=ubuf[:, 2:NTB + 2], in0=ubuf[:, 2:NTB + 2],
                                                   in1=mm[i][:, :], op=ALU.mult),
             reads=["u", "mm%d" % i], writes=["u"])

        def rest():
            P.op("dve", lambda e, cc=cc: e.tensor_scalar(
                out=acc[:], in0=ubuf[:, 2:NTB + 2], scalar1=cwh[:, 8 + cc:9 + cc], scalar2=cbh[:, cc:cc + 1],
                op0=ALU.mult, op1=ALU.add),
                reads=["u", "cwh", "cbh"], writes=["acc"])
            P.op("dve", lambda e, cc=cc: e.scalar_tensor_tensor(
                out=acc[:], in0=ubuf[:, 1:NTB + 1], scalar=cwh[:, 4 + cc:5 + cc], in1=acc[:],
                op0=ALU.mult, op1=ALU.add),
                reads=["u", "uh", "cwh", "acc"], writes=["acc"])
            P.op("dve", lambda e, cc=cc: e.scalar_tensor_tensor(
                out=acc[:], in0=ubuf[:, 0:NTB], scalar=cwh[:, cc:cc + 1], in1=acc[:],
                op0=ALU.mult, op1=ALU.add),
                reads=["u", "uh", "cwh", "acc"], writes=["acc"])
            P.op("dve", lambda e, cc=cc: e.tensor_copy(out=halo[:, cc, :], in_=ubuf[:, NTB:NTB + 2]),
                 reads=["u"], writes=["halo%d" % cc])
            P.op("dve", lambda e, cc=cc, tzv=tzv: e.tensor_tensor(out=yT[:, 4 + cc, :], in0=acc[:], in1=tzv,
                                                                   op=ALU.mult),
                 reads=["acc"] + tzr, writes=["yT%d" % (4 + cc)])
        if late is None:
            rest()
        else:
            late.append(rest)

    def stage_QK(b, h, has_prev):
        sl = b % 2
        pair = h // 2
        hh = h % 2
        hb = h % 2
        groups = [(0, 1), (2,), (3,), (4,), (5,), (6, 7)] if has_prev else [(4,), (5,), (6, 7)]
        for grp in groups:
            i = next_sc()
            coff = 0
            for j in grp:
                n = NQ[j] * 64
                ks = sl if j >= 4 else 1 - sl
                kt = j % 4
                extra = []
                q0 = max(2 * j - 8, LO[j])
                q1 = min(2 * j - 5, HI[j])
                if q1 >= q0:
                    c0 = q0 - (2 * j - 8)
                    nn = (q1 - q0 + 1) * 64
                    extra.append((coff + (q0 - LO[j]) * 64, nn, ("etab", h, c0 * 64)))
                if j <= 3:
                    extra.append((coff + (HI[j] - LO[j]) * 64, 64, ("mk",)))
                P.op("pe", lambda e, i=i, n=n, coff=coff, ks=ks, kt=kt, pair=pair, hh=hh, j=j,
                     sp=(len(extra) == 0): e.matmul(
                    sc[i][:, coff:coff + n], lhsT=kT[:, ks, pair, kt * 128:(kt + 1) * 128],
                    rhs=qz[:, pair, hh, LO[j] * 64:(HI[j] + 1) * 64], start=True, stop=sp),
                    reads=["kT%d_%d" % (ks, pair), "qz%d_%d" % (pair, hh)], writes=["sc%d" % i])
                yield
                for xi, (a0, nn, src) in enumerate(extra):
                    last = xi == len(extra) - 1
                    if src[0] == "etab":
                        P.op("pe", lambda e, i=i, a0=a0, nn=nn, hq=src[1], c=src[2], last=last: e.matmul(
                            sc[i][:, a0:a0 + nn], lhsT=ident[:], rhs=etab[:, hq, c:c + nn], start=False, stop=last),
                            reads=["ident", "etab"], writes=["sc%d" % i])
                    else:
                        P.op("pe", lambda e, i=i, a0=a0, last=last: e.matmul(
                            sc[i][:, a0:a0 + 64], lhsT=ident[:], rhs=mk[:], start=False, stop=last),
                            reads=["ident", "mk"], writes=["sc%d" % i])
                    yield
                coff += n
            j0 = grp[0]
            P.op("act", lambda e, i=i, coff=coff, hb=hb, j0=j0: e.activation(
                out=PT[:, hb, OFF[j0]:OFF[j0] + coff], in_=sc[i][:, 0:coff], func=AF.Exp, scale=0.125),
                reads=["sc%d" % i], writes=["PT%d_%d" % (hb, j) for j in grp])

    def stage_PV(b, h, has_prev, pend):
        sl = b % 2
        hb = h % 2
        pvb = pv[hb]
        for t in range(TPB):
            items = [j for j in range(t, t + 5) if has_prev or j >= 4]
            for idx, j in enumerate(items):
                ks = sl if j >= 4 else 1 - sl
                a0 = OFF[j] + (2 * t - LO[j]) * 64
                P.op("pe", lambda e, j=j, ks=ks, a0=a0, t=t, h=h, hb=hb, pvb=pvb,
                     st=(idx == 0), sp=(idx == len(items) - 1): e.matmul(
                    pvb[:, t, 0:65], lhsT=PT[:, hb, a0:a0 + 128],
                    rhs=vA[:, ks, j % 4, h, 0:65], start=st, stop=sp),
                    reads=["PT%d_%d" % (hb, j), "v%d_%d" % (ks, j % 4)], writes=["pv%d" % hb])
                yield

        def evac():
            P.op("dve", lambda e, pvb=pvb: e.reciprocal(out=rden[:], in_=pvb[:, :, 64]),
                 reads=["pv%d" % hb], writes=["rden"])
            P.op("dve", lambda e, pvb=pvb: e.scalar_tensor_tensor(
                out=ynorm[:], in0=pvb[:, :, 0:64], scalar=0.5,
                in1=rden[:, :].unsqueeze(2).to_broadcast([128, TPB, 64]), op0=ALU.mult, op1=ALU.mult),
                reads=["pv%d" % hb, "rden"], writes=["ynorm"])
            P.op("dve", lambda e, h=h: e.tensor_tensor(
                out=ytok[:, :, h * 64:(h + 1) * 64], in0=ynorm[:], in1=gate[:, :, h * 64:(h + 1) * 64],
                op=ALU.mult),
                reads=["ynorm"] + ["gate%d" % t for t in range(TPB)], writes=["ytok"])
        if h == H - 2:
            evac()
        else:
            pend.append(evac)

    def stage_attn(b, has_prev, nxt):
        first = not has_prev
        units = [unit_v(b, t) for t in range(TPB)] + [unit_za(t) for t in range(TPB)]
        late = []
        for cc in range(4):
            units += [unit_C(cc, first), unit_zc(cc), unit_B(cc), unit_h(cc, late if cc == 3 else None)]
        per_head = [5, 3, 3, 3, 3, 3, 2, 2]
        ui = 0
        pend = []
        for h in range(H):
            for f in pend:
                f()
            pend = []
            bg = []
            if b == 0 and h < 3:
                bg += [lambda n=n: w_item(n) for n in range(16 + 8 * h, 24 + 8 * h)]
                if h == 2 and nxt is not None:
                    bg.append(lambda: load_tile(1, 3))
            if h < 4 and b >= 1:
                bg.append(lambda h=h: out_tile_a(b - 1, h))

                def fin(h=h):
                    out_tile_b(b - 1, h)
                    if nxt is not None:
                        load_tile(nxt, h)
                bg.append(fin)
            if nxt is not None:
                if 4 <= h < 8:
                    ln0_act(nxt, h - 4)
                if 2 <= h < 6:
                    bg.append(lambda h=h: ln0_stats_k(nxt, h - 2, 0))

                    def st2(h=h):
                        ln0_stats_k(nxt, h - 2, 1)
                        ln0_aggr(nxt, h - 2)
                        ln0_small(nxt, h - 2)
                    bg.append(st2)
            streams = [(stage_QK(b, h, has_prev), 2), (chain(units[ui:ui + per_head[h]], bg), 3)]
            ui += per_head[h]
            if h >= 1:
                streams.append((stage_PV(b, h - 1, has_prev, pend), 3))
            interleave(streams)
            for f in bg:
                f()
        for f in pend:
            f()
        pend = []
        assert ui == len(units)
        drive(stage_PV(b, H - 1, has_prev, pend))
        for f in pend:
            f()
        for f in late:
            f()

    def stage_out(b, tgen):
        def tstep():
            if tgen is not None:
                next(tgen, None)
        tstep()
        tstep()
        for cp in range(2):
            i = next_mm()
            for kk in range(2):
                c = 2 * cp + kk
                for t in range(TPB):
                    P.op("pe", lambda e, i=i, c=c, kk=kk, t=t: e.transpose(
                        mmb[i][:, kk * 512 + t * 128: kk * 512 + (t + 1) * 128],
                        ytok[:, t, c * 128:(c + 1) * 128], ident[:]),
                        reads=["ytok", "ident"], writes=["mm%d" % i])
            for kk in range(2):
                c = 2 * cp + kk
                P.op("dve", lambda e, i=i, c=c, kk=kk: e.tensor_copy(
                    out=yT[:, c, :], in_=mmb[i][:, kk * 512:(kk + 1) * 512]),
                    reads=["mm%d" % i], writes=["yT%d" % c])
        for t in range(TPB):
            s = slot_of(b, t)
            xr = ["xs%d" % s]
            if t in (1, 2):
                tstep()
            for hf in range(2):
                i = next_mm()
                for c in range(8):
                    P.op("pe", lambda e, i=i, c=c, t=t, hf=hf: e.matmul(
                        mm[i][:, :], lhsT=yT[:, c, t * 128:(t + 1) * 128],
                        rhs=w_out_sb[:, c, hf * 512:(hf + 1) * 512], start=(c == 0), stop=(c == 7)),
                        reads=["w_out%d" % c, "yT%d" % c], writes=["mm%d" % i])
                P.op("dve", lambda e, i=i, s=s, hf=hf: e.tensor_tensor(
                    out=xs[:, s, hf * 512:(hf + 1) * 512], in0=xs[:, s, hf * 512:(hf + 1) * 512],
                    in1=mm[i][:, :], op=ALU.add),
                    reads=xr + ["mm%d" % i], writes=xr)
                P.op("dve", lambda e, t=t, s=s, hf=hf: e.bn_stats(out=stats1[:, t, hf, :],
                                                                  in_=xs[:, s, hf * 512:(hf + 1) * 512]),
                     reads=xr, writes=["stats1_%d" % t])
            P.op("dve", lambda e, t=t: e.bn_aggr(out=mv1[:, t, :],
                                                 in_=stats1[:, t, :, :].rearrange("p a b -> p (a b)")),
                 reads=["stats1_%d" % t], writes=["mv1"])

    def stage_out_pool(b):
        P.op("pool", lambda e: e.tensor_scalar(out=rstd1[:], in0=mv1[:, :, 1], scalar1=LN_EPS, scalar2=None,
                                               op0=ALU.add),
             reads=["mv1"], writes=["rstd1"])
        P.op("pool", lambda e: e.tensor_tensor(out=rstd1[:], in0=rstd1[:], in1=nhalf[:], op=ALU.pow),
             reads=["rstd1", "nhalf"], writes=["rstd1"])
        P.op("pool", lambda e: e.tensor_scalar(out=nmean1[:], in0=mv1[:, :, 0], scalar1=-1.0, scalar2=None,
                                               op0=ALU.mult),
             reads=["mv1"], writes=["nmean1"])

    def out_tile_a(b, t):
        s = slot_of(b, t)
        xr = ["xs%d" % s]
        P.op("dve", lambda e, t=t, s=s: e.scalar_tensor_tensor(
            out=xs[:, s, :], in0=xs[:, s, :], scalar=nmean1[:, t:t + 1], in1=g1bc[:],
            op0=ALU.add, op1=ALU.mult),
            reads=xr + ["nmean1", "g1bc"], writes=xr)

    def out_tile(b, t):
        out_tile_a(b, t)
        out_tile_b(b, t)

    def out_tile_b(b, t):
        s = slot_of(b, t)
        n = b * TPB + t
        xr = ["xs%d" % s]
        P.op("dve", lambda e, t=t, s=s: e.scalar_tensor_tensor(
            out=xs[:, s, :], in0=xs[:, s, :], scalar=rstd1[:, t:t + 1], in1=b1bc[:],
            op0=ALU.mult, op1=ALU.add),
            reads=xr + ["rstd1", "b1bc"], writes=xr)
        P.dma("sp", lambda e, n=n, s=s: e.dma_start(out=out_d[n], in_=xs[:, s, :]),
              "st%d" % s, reads=xr)

    stage_load(0)
    w_dma(0)
    w_dma(1)
    stage_ln0(0)
    drive(stage_T(0))
    for n in range(8):
        w_item(n)
    drive(stage_P1(0))
    for n in range(8, 16):
        w_item(n)
    if nblk > 1:
        for t in range(3):
            load_tile(1, t)
    for b in range(nblk):
        has_prev = (b % nbs) != 0
        nxt = b + 1 if b + 1 < nblk else None
        stage_attn(b, has_prev, nxt)
        stage_out(b, stage_T(nxt) if nxt is not None else None)
        stage_out_pool(b)
        if nxt is not None:
            rb = []
            for t in range(TPB):
                rb += [lambda t=t: ln0_res_a(nxt, t), lambda t=t: ln0_res_b(nxt, t)]
            drive(stage_P1(nxt, rb))
            for f in rb:
                f()
    for t in range(TPB):
        out_tile(nblk - 1, t)
    P.wait_all("sp", ["xs%d" % s for s in range(NSLOT)])

    stats = P.emit(nc, es)
    stats["sbuf_free"] = nc.sbuf_bytes_remaining
    stats["prog"] = P
    es.close()
    return nc, stats


def _host_layout(ln0_g, ln0_b, w_in, rel_bias, conv_w, conv_b, w_out, ln_g, ln_b):
    f = np.float32
    kp = np.arange(128)[:, None, None]
    c = np.arange(4)[None, :, None]
    i = np.arange(64)[None, None, :]
    dist = (c - kp // 64) * 64 + i - (kp % 64)
    idx = np.clip(dist, -128, 128) + 128
    rb = np.asarray(rel_bias[0], f)
    rtab = np.ascontiguousarray(rb[:, idx].transpose(1, 0, 2, 3)).reshape(128, 2048)
    rc = np.ascontiguousarray(np.broadcast_to(rb[:, 256][None, :], (128, 8)))
    cw = np.asarray(conv_w[0], f)
    cwl = np.ascontiguousarray(cw.reshape(3, 4, 128).transpose(2, 0, 1)).reshape(128, 12)
    cbl = np.ascontiguousarray(np.asarray(conv_b[0], f).reshape(4, 128).T)
    bc = lambda v: np.ascontiguousarray(np.broadcast_to(np.asarray(v, f).reshape(1, D), (128, D)))
    return {
        "w_in": np.ascontiguousarray(np.asarray(w_in[0], f)),
        "w_out": np.ascontiguousarray(np.asarray(w_out[0], f)),
        "g0T": np.ascontiguousarray(np.asarray(ln0_g, f).reshape(8, 128).T),
        "b0T": np.ascontiguousarray(np.asarray(ln0_b, f).reshape(8, 128).T),
        "g0bc": bc(ln0_g), "b0bc": bc(ln0_b), "g1bc": bc(ln_g[0]), "b1bc": bc(ln_b[0]),
        "cw": cwl, "cb": cbl, "rtab": rtab, "rc": rc,
    }


def kernel(x, ln0_g, ln0_b, w_in, rel_bias, conv_w, conv_b, w_out, ln_g, ln_b):
    x = np.asarray(x, np.float32)
    bsz, seqlen, d = x.shape
    nseq = bsz // N_CORES
    nc, _ = build(nseq, seqlen)
    shared = _host_layout(ln0_g, ln0_b, w_in, rel_bias, conv_w, conv_b, w_out, ln_g, ln_b)
    in_maps = []
    for c in range(N_CORES):
        m = dict(shared)
        m["x"] = np.ascontiguousarray(x[c * nseq:(c + 1) * nseq].reshape(nseq * seqlen, d))
        in_maps.append(m)
    res = run_bass_kernel_spmd(nc, in_maps, core_ids=list(range(N_CORES)))
    outs = [np.asarray(r["out"], np.float32).reshape(nseq, seqlen, d) for r in res.results]
    return np.concatenate(outs, axis=0)
```

```python
import numpy as np
from contextlib import ExitStack

import concourse.bass as bass
import concourse.mybir as mybir
from concourse.bass_utils import run_bass_kernel_spmd

F32 = mybir.dt.float32
BF16 = mybir.dt.bfloat16
ALU = mybir.AluOpType
AF = mybir.ActivationFunctionType

N_CORES = 8
D = 1024
NCOLS = 4096
H = 8
DH = 64
NTB = 512
TPB = 4
LN_EPS = 1e-5
ALPHA = 2.0 ** 0.25
NSLOT = 8
ENGS = ("pe", "act", "dve", "pool", "sp")

LO = [max(0, 2 * j - 8) for j in range(8)]
HI = [min(7, 2 * j + 1) for j in range(8)]
NQ = [HI[j] - LO[j] + 1 for j in range(8)]
OFF = [64 * sum(NQ[:j]) for j in range(8)]
PTW = 64 * sum(NQ)


class Prog:
    def __init__(self):
        self.q = {e: [] for e in ENGS}
        self.res = {}
        self.dma_cnt = {}

    def _deps(self, reads, writes):
        deps = []
        for r in reads:
            st = self.res.get(r)
            if st and st[0] is not None:
                deps.append(("raw", st[0]))
        for w in writes:
            st = self.res.get(w)
            if st:
                if st[0] is not None:
                    deps.append(("waw", st[0]))
                for ev in st[1].values():
                    deps.append(("war", ev))
        return deps

    def _commit(self, ev, reads, writes):
        for r in reads:
            st = self.res.setdefault(r, [None, {}])
            st[1][ev[1]] = ev
        for w in writes:
            self.res[w] = [ev, {}]

    def op(self, eng, fn, reads=(), writes=()):
        idx = len(self.q[eng])
        self.q[eng].append(dict(fn=fn, deps=self._deps(reads, writes), mark=False, dma=None,
                                tag="%s<-%s" % (",".join(writes), ",".join(reads))))
        self._commit(("e", eng, idx), reads, writes)

    def dma(self, eng, fn, key, reads=(), writes=()):
        val = self.dma_cnt.get(key, 0) + 16
        self.dma_cnt[key] = val
        self.q[eng].append(dict(fn=fn, deps=self._deps(reads, writes), mark=False, dma=(key, val)))
        self._commit(("d", key, val), reads, writes)

    def fix_dma_key(self, key):
        tot = self.dma_cnt[key]
        for st in self.res.values():
            if st[0] is not None and st[0][0] == "d" and st[0][1] == key:
                st[0] = ("d", key, tot)

    def wait_all(self, eng, names):
        deps = []
        for n in names:
            st = self.res.get(n)
            if st:
                if st[0] is not None:
                    deps.append(("raw", st[0]))
                for ev in st[1].values():
                    deps.append(("war", ev))
        self.q[eng].append(dict(fn=None, deps=deps, mark=False, dma=None))

    @staticmethod
    def _skip(kind, ev, e):
        return ev[0] == "e" and ev[1] == e and e == "pe"

    def emit(self, nc, es):
        for e in ENGS:
            for ins in self.q[e]:
                for kind, ev in ins["deps"]:
                    if ev[0] == "e" and not self._skip(kind, ev, e):
                        self.q[ev[1]][ev[2]]["mark"] = True
        for e in ENGS:
            c = 0
            for ins in self.q[e]:
                if ins["mark"]:
                    c += 1
                    ins["cnt"] = c
        esem = {e: es.enter_context(nc.semaphore("s_" + e)) for e in ENGS}
        dsem = {k: es.enter_context(nc.semaphore("d_" + k)) for k in self.dma_cnt}
        stats = {e: [len(self.q[e]), 0] for e in ENGS}

        def run(e, eng):
            seen = {}
            for ins in self.q[e]:
                need = {}
                for kind, ev in ins["deps"]:
                    if self._skip(kind, ev, e):
                        continue
                    if ev[0] == "e":
                        k = ("e", ev[1])
                        v = self.q[ev[1]][ev[2]]["cnt"]
                    else:
                        k = ("d", ev[1])
                        v = ev[2]
                    if v > need.get(k, 0):
                        need[k] = v
                for k, v in need.items():
                    if seen.get(k, 0) >= v:
                        continue
                    seen[k] = v
                    eng.wait_ge(esem[k[1]] if k[0] == "e" else dsem[k[1]], v)
                    stats[e][1] += 1
                if ins["fn"] is None:
                    continue
                r = ins["fn"](eng)
                if ins["dma"] is not None:
                    r.then_inc(dsem[ins["dma"][0]], 16)
                elif ins["mark"]:
                    r.then_inc(esem[e], 1)

        block = es.enter_context(nc.Block())

        @block.tensor
        def _(eng):
            run("pe", eng)

        @block.scalar
        def _(eng):
            run("act", eng)

        @block.vector
        def _(eng):
            run("dve", eng)

        @block.gpsimd
        def _(eng):
            run("pool", eng)

        @block.sync
        def _(eng):
            run("sp", eng)

        return stats


def build(nseq, seqlen):
    ntok = nseq * seqlen
    nbs = seqlen // NTB
    nblk = nseq * nbs
    ntile = ntok // 128

    nc = bass.Bass("TRN2", target_bir_lowering=False)
    es = ExitStack()

    def din(name, shape):
        return nc.dram_tensor(name, shape, F32, kind="ExternalInput").ap()

    x_d = din("x", [ntok, D]).rearrange("(n p) d -> n p d", p=128)
    w_in_d = din("w_in", [D, NCOLS])
    w_out_d = din("w_out", [D, D])
    g0T_d = din("g0T", [128, 8])
    b0T_d = din("b0T", [128, 8])
    g0bc_d = din("g0bc", [128, D])
    b0bc_d = din("b0bc", [128, D])
    g1bc_d = din("g1bc", [128, D])
    b1bc_d = din("b1bc", [128, D])
    cw_d = din("cw", [128, 12])
    cb_d = din("cb", [128, 4])
    rtab_d = din("rtab", [128, 2048])
    rc_d = din("rc", [128, 8])
    out_d = nc.dram_tensor("out", [ntok, D], F32, kind="ExternalOutput").ap().rearrange(
        "(n p) d -> n p d", p=128)

    def sb(name, shape, dt):
        return es.enter_context(nc.sbuf_tensor(name, shape, dt))

    def ps(name, shape, dt):
        return es.enter_context(nc.psum_tensor(name, shape, dt))

    w_in_sb = sb("w_in_sb", [128, 8, NCOLS], BF16)
    w_out_sb = sb("w_out_sb", [128, 8, D], BF16)
    ag0 = sb("ag0", [128, D], F32)
    ab0 = sb("ab0", [128, D], F32)
    g1bc = sb("g1bc_sb", [128, D], F32)
    b1bc = sb("b1bc_sb", [128, D], F32)
    xs = sb("xs", [128, NSLOT, D], F32)
    xhb = sb("xhb", [128, TPB, D], BF16)
    xnT = sb("xnT", [128, 8, NTB], BF16)
    qz = sb("qz", [128, 4, 2, NTB], BF16)
    kT = sb("kT", [128, 2, 4, NTB], BF16)
    vA = sb("vA", [128, 2, TPB, H, 65], BF16)
    PT = sb("PT", [128, 2, PTW], BF16)
    etab = sb("etab", [128, H, 256], BF16)
    gate = sb("gate", [128, TPB, 512], BF16)
    ytok = sb("ytok", [128, TPB, 512], BF16)
    ynorm = sb("ynorm", [128, TPB, 64], F32)
    yT = sb("yT", [128, 8, NTB], BF16)
    ubuf = sb("ubuf", [128, NTB + 2], F32)
    acc = sb("acc", [128, NTB], F32)
    tz = sb("tz", [128, NTB], F32)
    halo = sb("halo", [128, 4, 2], F32)
    ident = sb("ident", [128, 128], BF16)
    mk = sb("mk", [128, 64], BF16)
    g0T = sb("g0T_sb", [128, 8], F32)
    b0T = sb("b0T_sb", [128, 8], F32)
    cwh = sb("cwh", [128, 12], F32)
    cbh = sb("cbh", [128, 4], F32)
    nrc = sb("nrc", [128, 8], F32)
    nhalf = sb("nhalf", [128, 4], F32)
    stats0 = sb("stats0", [128, TPB, 2, 6], F32)
    mv0 = sb("mv0", [128, TPB, 2], F32)
    rstd0 = sb("rstd0", [128, TPB], F32)
    nmean0 = sb("nmean0", [128, TPB], F32)
    nmr0 = sb("nmr0", [128, TPB], F32)
    stats1 = sb("stats1", [128, TPB, 2, 6], F32)
    mv1 = sb("mv1", [128, TPB, 2], F32)
    rstd1 = sb("rstd1", [128, TPB], F32)
    nmean1 = sb("nmean1", [128, TPB], F32)
    rden = sb("rden", [128, TPB], F32)

    NMM = 3
    NSC = 3
    mm = [ps("mm%d" % i, [128, 512], F32) for i in range(NMM)]
    mmb = [m[:, :].bitcast(BF16) for m in mm]
    sc = [ps("sc%d" % i, [128, 512], F32) for i in range(NSC)]
    pv = [ps("pv%d" % i, [128, TPB, 65], F32) for i in range(2)]

    P = Prog()
    mmc = [0]
    scc = [0]

    def next_mm():
        i = mmc[0] % NMM
        mmc[0] += 1
        return i

    def next_sc():
        i = scc[0] % NSC
        scc[0] += 1
        return i

    for dst, src, nm in ((g0T, g0T_d, "g0T"), (b0T, b0T_d, "b0T"), (ag0, g0bc_d, "ag0"),
                         (ab0, b0bc_d, "ab0"), (g1bc, g1bc_d, "g1bc"), (b1bc, b1bc_d, "b1bc"),
                         (cwh, cw_d, "cwh"), (cbh, cb_d, "cbh"), (nrc, rc_d, "nrc")):
        P.dma("sp", lambda e, dst=dst, src=src: e.dma_start(out=dst[:], in_=src), "c", writes=[nm])
    P.dma("sp", lambda e: e.dma_start(out=xs[:, 6:8, :].rearrange("p a d -> p (a d)"), in_=rtab_d),
          "c", writes=["xs6", "xs7"])
    P.fix_dma_key("c")

    P.op("pool", lambda e: e.memset(ident[:], 0.0), writes=["ident"])
    P.op("pool", lambda e: e.affine_select(out=ident[:], in_=ident[:], pattern=[[-1, 128]],
                                           compare_op=ALU.not_equal, fill=1.0, base=0,
                                           channel_multiplier=1),
         reads=["ident"], writes=["ident"])
    P.op("pool", lambda e: e.memset(nhalf[:], -0.5), writes=["nhalf"])
    vres = ["v%d_%d" % (s, t) for s in range(2) for t in range(TPB)]
    P.op("dve", lambda e: e.memset(vA[:], 1.0), writes=vres)
    P.op("dve", lambda e: e.memset(qz[:], 0.0), writes=["qz%d_%d" % (p, hh) for p in range(4) for hh in range(2)])
    P.op("dve", lambda e: e.tensor_scalar(out=ag0[:], in0=ag0[:], scalar1=ALPHA, scalar2=None, op0=ALU.mult),
         reads=["ag0"], writes=["ag0"])
    P.op("dve", lambda e: e.tensor_scalar(out=ab0[:], in0=ab0[:], scalar1=ALPHA, scalar2=None, op0=ALU.mult),
         reads=["ab0"], writes=["ab0"])
    P.op("dve", lambda e: e.tensor_scalar(out=cwh[:], in0=cwh[:], scalar1=0.5, scalar2=None, op0=ALU.mult),
         reads=["cwh"], writes=["cwh"])
    P.op("dve", lambda e: e.tensor_scalar(out=cbh[:], in0=cbh[:], scalar1=0.5, scalar2=None, op0=ALU.mult),
         reads=["cbh"], writes=["cbh"])
    P.op("dve", lambda e: e.tensor_scalar(out=nrc[:], in0=nrc[:], scalar1=-1.0, scalar2=None, op0=ALU.mult),
         reads=["nrc"], writes=["nrc"])
    rtab_sb = xs[:, 6:8, :].rearrange("p a d -> p (a d)")
    NEG = -240000.0
    for h in range(H):
        P.op("dve", lambda e, h=h: e.tensor_scalar(out=etab[:, h, :], in0=rtab_sb[:, h * 256:(h + 1) * 256],
                                                   scalar1=nrc[:, h:h + 1], scalar2=8.0, op0=ALU.add, op1=ALU.mult),
             reads=["xs6", "xs7", "nrc"], writes=["etab"])
    P.op("pool", lambda e: e.memset(etab[64:128, :, 0:64], NEG), reads=["etab"], writes=["etab"])
    P.op("pool", lambda e: e.memset(mk[:], 0.0), writes=["mk"])
    P.op("pool", lambda e: e.memset(mk[0:64, :], NEG), reads=["mk"], writes=["mk"])

    yT_f = yT[:, :, :].rearrange("p a b -> p (a b)").bitcast(F32)
    ytok_f = ytok[:, :, :].rearrange("p a b -> p (a b)").bitcast(F32)
    stg = [yT_f[:, 0:1024], yT_f[:, 1024:2048], ytok_f[:, 0:1024]]
    stg_res = [["yT%d" % c for c in range(4)], ["yT%d" % c for c in range(4, 8)], ["ytok"]]

    wchunks = []
    for g in range(4):
        for kc in range(8):
            wchunks.append((w_in_d[kc * 128:(kc + 1) * 128, g * 1024:(g + 1) * 1024],
                            w_in_sb[:, kc, g * 1024:(g + 1) * 1024], "w_in%d_%d" % (g, kc)))
    for kc in range(8):
        wchunks.append((w_out_d[kc * 128:(kc + 1) * 128, :], w_out_sb[:, kc, :], "w_out%d" % kc))
    NWC = len(wchunks)

    xhb_f = xhb[:, :, :].rearrange("p a b -> p (a b)").bitcast(F32)
    stg2 = [xhb_f[:, 0:1024], xhb_f[:, 1024:2048], xs[:, 7, :]]
    stg2_res = [["xhb0", "xhb1"], ["xhb2", "xhb3"], ["xs7"]]

    def w_stage(n):
        k = n % 3
        return (stg[k], stg_res[k], "ws%d" % k) if n < 16 else (stg2[k], stg2_res[k], "wt%d" % k)

    def w_dma(n):
        buf, res, key = w_stage(n)
        src = wchunks[n][0]
        P.dma("sp", lambda e, buf=buf, src=src: e.dma_start(out=buf, in_=src), key, writes=res)

    def w_cast(n):
        buf, bres, key = w_stage(n)
        dst, res = wchunks[n][1], wchunks[n][2]
        P.op("dve", lambda e, buf=buf, dst=dst: e.tensor_copy(out=dst, in_=buf), reads=bres, writes=[res])

    def w_item(n):
        if n + 2 < NWC:
            w_dma(n + 2)
        w_cast(n)

    def slot_of(b, t):
        return (b * TPB + t) % NSLOT

    def load_tile(b, t):
        n = b * TPB + t
        s = slot_of(b, t)
        P.dma("sp", lambda e, n=n, s=s: e.dma_start(out=xs[:, s, :], in_=x_d[n]),
              "xl%d" % s, writes=["xs%d" % s])

    def stage_load(b):
        for t in range(TPB):
            load_tile(b, t)

    def ln0_stats_k(b, t, k):
        s = slot_of(b, t)
        P.op("dve", lambda e, t=t, s=s, k=k: e.bn_stats(out=stats0[:, t, k, :],
                                                        in_=xs[:, s, k * 512:(k + 1) * 512]),
             reads=["xs%d" % s], writes=["stats0_%d" % t])

    def ln0_aggr(b, t):
        P.op("dve", lambda e, t=t: e.bn_aggr(out=mv0[:, t, :],
                                             in_=stats0[:, t, :, :].rearrange("p a b -> p (a b)")),
             reads=["stats0_%d" % t], writes=["mv0_%d" % t])

    def ln0_stats(b, t):
        ln0_stats_k(b, t, 0)
        ln0_stats_k(b, t, 1)
        ln0_aggr(b, t)

    def ln0_small(b, t):
        P.op("pool", lambda e, t=t: e.tensor_scalar(out=rstd0[:, t:t + 1], in0=mv0[:, t, 1:2], scalar1=LN_EPS,
                                                    scalar2=None, op0=ALU.add),
             reads=["mv0_%d" % t], writes=["rstd0_%d" % t])
        P.op("pool", lambda e, t=t: e.tensor_tensor(out=rstd0[:, t:t + 1], in0=rstd0[:, t:t + 1],
                                                    in1=nhalf[:, 0:1], op=ALU.pow),
             reads=["rstd0_%d" % t, "nhalf"], writes=["rstd0_%d" % t])
        P.op("pool", lambda e, t=t: e.tensor_scalar(out=nmean0[:, t:t + 1], in0=mv0[:, t, 0:1], scalar1=-1.0,
                                                    scalar2=None, op0=ALU.mult),
             reads=["mv0_%d" % t], writes=["nmean0_%d" % t])
        P.op("pool", lambda e, t=t: e.tensor_tensor(out=nmr0[:, t:t + 1], in0=nmean0[:, t:t + 1],
                                                    in1=rstd0[:, t:t + 1], op=ALU.mult),
             reads=["nmean0_%d" % t, "rstd0_%d" % t], writes=["nmr0_%d" % t])

    def ln0_act(b, t):
        s = slot_of(b, t)
        P.op("act", lambda e, t=t, s=s: e.activation(out=xhb[:, t, :], in_=xs[:, s, :], func=AF.Identity,
                                                     bias=nmr0[:, t:t + 1], scale=rstd0[:, t:t + 1]),
             reads=["xs%d" % s, "nmr0_%d" % t, "rstd0_%d" % t], writes=["xhb%d" % t])

    def ln0_res_a(b, t):
        s = slot_of(b, t)
        xr = ["xs%d" % s]
        P.op("dve", lambda e, t=t, s=s: e.scalar_tensor_tensor(
            out=xs[:, s, :], in0=xs[:, s, :], scalar=nmean0[:, t:t + 1], in1=ag0[:],
            op0=ALU.add, op1=ALU.mult),
            reads=xr + ["nmean0_%d" % t, "ag0"], writes=xr)

    def ln0_res_b(b, t):
        s = slot_of(b, t)
        xr = ["xs%d" % s]
        P.op("dve", lambda e, t=t, s=s: e.scalar_tensor_tensor(
            out=xs[:, s, :], in0=xs[:, s, :], scalar=rstd0[:, t:t + 1], in1=ab0[:],
            op0=ALU.mult, op1=ALU.add),
            reads=xr + ["rstd0_%d" % t, "ab0"], writes=xr)

    def ln0_res(b, t):
        ln0_res_a(b, t)
        ln0_res_b(b, t)

    def stage_ln0(b):
        for t in range(TPB):
            ln0_stats(b, t)
            ln0_small(b, t)
        for t in range(TPB):
            ln0_act(b, t)
        for t in range(TPB):
            ln0_res(b, t)

    def stage_T(b):
        for kp in range(4):
            i = next_mm()
            for kk in range(2):
                kc = 2 * kp + kk
                for t in range(TPB):
                    P.op("pe", lambda e, i=i, kc=kc, kk=kk, t=t: e.transpose(
                        mmb[i][:, kk * 512 + t * 128: kk * 512 + (t + 1) * 128],
                        xhb[:, t, kc * 128:(kc + 1) * 128], ident[:]),
                        reads=["xhb%d" % t, "ident"], writes=["mm%d" % i])
            for kk in range(2):
                kc = 2 * kp + kk
                P.op("act", lambda e, i=i, kc=kc, kk=kk: e.activation(
                    out=xnT[:, kc, :], in_=mmb[i][:, kk * 512:(kk + 1) * 512], func=AF.Identity,
                    bias=b0T[:, kc:kc + 1], scale=g0T[:, kc:kc + 1]),
                    reads=["mm%d" % i, "g0T", "b0T"], writes=["xnT%d" % kc])
            yield

    xnT_all = ["xnT%d" % kc for kc in range(8)]

    def proj_fm(cc):
        i = next_mm()
        for kc in range(8):
            P.op("pe", lambda e, i=i, kc=kc, cc=cc: e.matmul(
                mm[i][:, :], lhsT=w_in_sb[:, kc, cc * 128:(cc + 1) * 128], rhs=xnT[:, kc, :],
                start=(kc == 0), stop=(kc == 7)),
                reads=["w_in%d_%d" % (cc // 8, kc), "xnT%d" % kc], writes=["mm%d" % i])
            yield
        return i

    def proj_tm(t, c0):
        i = next_mm()
        for kc in range(8):
            P.op("pe", lambda e, i=i, kc=kc, t=t, c0=c0: e.matmul(
                mm[i][:, :], lhsT=xnT[:, kc, t * 128:(t + 1) * 128], rhs=w_in_sb[:, kc, c0:c0 + 512],
                start=(kc == 0), stop=(kc == 7)),
                reads=["w_in%d_%d" % (c0 // 1024, kc), "xnT%d" % kc], writes=["mm%d" % i])
            yield
        return i

    def drive(gen):
        for _ in gen:
            pass

    def interleave(streams):
        active = [[g, w] for g, w in streams]
        while active:
            for s in list(active):
                for _ in range(s[1]):
                    try:
                        next(s[0])
                    except StopIteration:
                        active.remove(s)
                        break

    def chain(gens, bg=None):
        n = len(gens)
        for k, g in enumerate(gens):
            yield from g
            if bg:
                take = -(-len(bg) // (n - k))
                for _ in range(take):
                    bg.pop(0)()

    def stage_P1(b, bg=None):
        sl = b % 2
        bg = bg or []
        for p in range(4):
            i = yield from proj_fm(p)
            P.op("act", lambda e, i=i, p=p: e.activation(
                out=qz[0:64, p, 0, :], in_=mm[i][0:64, :], func=AF.Copy),
                reads=["mm%d" % i], writes=["qz%d_0" % p])
            P.op("act", lambda e, i=i, p=p: e.activation(
                out=qz[64:128, p, 1, :], in_=mm[i][64:128, :], func=AF.Copy),
                reads=["mm%d" % i], writes=["qz%d_1" % p])
            if bg:
                bg.pop(0)()
            i = yield from proj_fm(4 + p)
            P.op("act", lambda e, i=i, p=p, sl=sl: e.activation(out=kT[:, sl, p, :], in_=mm[i][:, :], func=AF.Copy),
                 reads=["mm%d" % i], writes=["kT%d_%d" % (sl, p)])
            if bg:
                bg.pop(0)()

    def unit_v(b, t):
        sl = b % 2
        i = yield from proj_tm(t, 1024)
        P.op("act", lambda e, i=i, t=t, sl=sl: e.activation(
            out=vA[:, sl, t, :, 0:64], in_=mm[i][:, :].rearrange("p (h d) -> p h d", d=64),
            func=AF.Copy),
            reads=["mm%d" % i], writes=["v%d_%d" % (sl, t)])

    def unit_za(t):
        i = yield from proj_tm(t, 1536)
        P.op("act", lambda e, i=i: e.activation(out=tz[:], in_=mm[i][:, :], func=AF.Tanh, scale=0.5),
             reads=["mm%d" % i], writes=["tz"])
        P.op("dve", lambda e, i=i, t=t: e.scalar_tensor_tensor(
            out=gate[:, t, :], in0=tz[:], scalar=1.0, in1=mm[i][:, :], op0=ALU.add, op1=ALU.mult),
            reads=["tz", "mm%d" % i], writes=["gate%d" % t])

    tzb = [(tz[:], ["tz"]), (yT_f[:, 0:NTB], ["yT0", "yT1"])]

    def unit_C(cc, first):
        i = yield from proj_fm(20 + cc)
        P.op("act", lambda e, i=i: e.activation(out=ubuf[:, 2:NTB + 2], in_=mm[i][:, :], func=AF.Copy),
             reads=["mm%d" % i], writes=["u"])
        if first:
            P.op("dve", lambda e: e.memset(ubuf[:, 0:2], 0.0), writes=["uh"])
        else:
            P.op("dve", lambda e, cc=cc: e.tensor_copy(out=ubuf[:, 0:2], in_=halo[:, cc, :]),
                 reads=["halo%d" % cc], writes=["uh"])

    def unit_zc(cc):
        tzv, tzr = tzb[(cc + 1) % 2]
        i = yield from proj_fm(28 + cc)
        P.op("act", lambda e, i=i, tzv=tzv: e.activation(out=tzv, in_=mm[i][:, :], func=AF.Tanh, scale=0.5),
             reads=["mm%d" % i], writes=tzr)
        P.op("dve", lambda e, i=i, tzv=tzv: e.scalar_tensor_tensor(
            out=tzv, in0=tzv, scalar=1.0, in1=mm[i][:, :], op0=ALU.add, op1=ALU.mult),
            reads=tzr + ["mm%d" % i], writes=tzr)

    def unit_B(cc):
        tzv, tzr = tzb[(cc + 1) % 2]
        i = yield from proj_fm(16 + cc)
        P.op("dve", lambda e, i=i, tzv=tzv: e.tensor_tensor(out=tzv, in0=tzv, in1=mm[i][:, :], op=ALU.mult),
             reads=tzr + ["mm%d" % i], writes=tzr)

    def unit_h(cc, late=None):
        tzv, tzr = tzb[(cc + 1) % 2]
        i = yield from proj_fm(24 + cc)
        P.op("dve", lambda e, i=i: e.tensor_tensor(out=ubuf[:, 2:NTB + 2], in0=ubuf[:, 2:NTB + 2],
                                                   in1=mm[i][:, :], op=ALU.mult),
             reads=["u", "mm%d" % i], writes=["u"])

        def rest():
            P.op("dve", lambda e, cc=cc: e.tensor_scalar(
                out=acc[:], in0=ubuf[:, 2:NTB + 2], scalar1=cwh[:, 8 + cc:9 + cc], scalar2=cbh[:, cc:cc + 1],
                op0=ALU.mult, op1=ALU.add),
                reads=["u", "cwh", "cbh"], writes=["acc"])
            P.op("dve", lambda e, cc=cc: e.scalar_tensor_tensor(
                out=acc[:], in0=ubuf[:, 1:NTB + 1], scalar=cwh[:, 4 + cc:5 + cc], in1=acc[:],
                op0=ALU.mult, op1=ALU.add),
                reads=["u", "uh", "cwh", "acc"], writes=["acc"])
            P.op("dve", lambda e, cc=cc: e.scalar_tensor_tensor(
                out=acc[:], in0=ubuf[:, 0:NTB], scalar=cwh[:, cc:cc + 1], in1=acc[:],
                op0=ALU.mult, op1=ALU.add),
                reads=["u", "uh", "cwh", "acc"], writes=["acc"])
            P.op("dve", lambda e, cc=cc: e.tensor_copy(out=halo[:, cc, :], in_=ubuf[:, NTB:NTB + 2]),
                 reads=["u"], writes=["halo%d" % cc])
            P.op("dve", lambda e, cc=cc, tzv=tzv: e.tensor_tensor(out=yT[:, 4 + cc, :], in0=acc[:], in1=tzv,
                                                                   op=ALU.mult),
                 reads=["acc"] + tzr, writes=["yT%d" % (4 + cc)])
        if late is None:
            rest()
        else:
            late.append(rest)

    def stage_QK(b, h, has_prev):
        sl = b % 2
        pair = h // 2
        hh = h % 2
        hb = h % 2
        groups = [(0, 1), (2,), (3,), (4,), (5,), (6, 7)] if has_prev else [(4,), (5,), (6, 7)]
        for grp in groups:
            i = next_sc()
            coff = 0
            for j in grp:
                n = NQ[j] * 64
                ks = sl if j >= 4 else 1 - sl
                kt = j % 4
                extra = []
                q0 = max(2 * j - 8, LO[j])
                q1 = min(2 * j - 5, HI[j])
                if q1 >= q0:
                    c0 = q0 - (2 * j - 8)
                    nn = (q1 - q0 + 1) * 64
                    extra.append((coff + (q0 - LO[j]) * 64, nn, ("etab", h, c0 * 64)))
                if j <= 3:
                    extra.append((coff + (HI[j] - LO[j]) * 64, 64, ("mk",)))
                P.op("pe", lambda e, i=i, n=n, coff=coff, ks=ks, kt=kt, pair=pair, hh=hh, j=j,
                     sp=(len(extra) == 0): e.matmul(
                    sc[i][:, coff:coff + n], lhsT=kT[:, ks, pair, kt * 128:(kt + 1) * 128],
                    rhs=qz[:, pair, hh, LO[j] * 64:(HI[j] + 1) * 64], start=True, stop=sp),
                    reads=["kT%d_%d" % (ks, pair), "qz%d_%d" % (pair, hh)], writes=["sc%d" % i])
                yield
                for xi, (a0, nn, src) in enumerate(extra):
                    last = xi == len(extra) - 1
                    if src[0] == "etab":
                        P.op("pe", lambda e, i=i, a0=a0, nn=nn, hq=src[1], c=src[2], last=last: e.matmul(
                            sc[i][:, a0:a0 + nn], lhsT=ident[:], rhs=etab[:, hq, c:c + nn], start=False, stop=last),
                            reads=["ident", "etab"], writes=["sc%d" % i])
                    else:
                        P.op("pe", lambda e, i=i, a0=a0, last=last: e.matmul(
                            sc[i][:, a0:a0 + 64], lhsT=ident[:], rhs=mk[:], start=False, stop=last),
                            reads=["ident", "mk"], writes=["sc%d" % i])
                    yield
                coff += n
            j0 = grp[0]
            P.op("act", lambda e, i=i, coff=coff, hb=hb, j0=j0: e.activation(
                out=PT[:, hb, OFF[j0]:OFF[j0] + coff], in_=sc[i][:, 0:coff], func=AF.Exp, scale=0.125),
                reads=["sc%d" % i], writes=["PT%d_%d" % (hb, j) for j in grp])

    def stage_PV(b, h, has_prev, pend):
        sl = b % 2
        hb = h % 2
        pvb = pv[hb]
        for t in range(TPB):
            items = [j for j in range(t, t + 5) if has_prev or j >= 4]
            for idx, j in enumerate(items):
                ks = sl if j >= 4 else 1 - sl
                a0 = OFF[j] + (2 * t - LO[j]) * 64
                P.op("pe", lambda e, j=j, ks=ks, a0=a0, t=t, h=h, hb=hb, pvb=pvb,
                     st=(idx == 0), sp=(idx == len(items) - 1): e.matmul(
                    pvb[:, t, 0:65], lhsT=PT[:, hb, a0:a0 + 128],
                    rhs=vA[:, ks, j % 4, h, 0:65], start=st, stop=sp),
                    reads=["PT%d_%d" % (hb, j), "v%d_%d" % (ks, j % 4)], writes=["pv%d" % hb])
                yield

        def evac():
            P.op("dve", lambda e, pvb=pvb: e.reciprocal(out=rden[:], in_=pvb[:, :, 64]),
                 reads=["pv%d" % hb], writes=["rden"])
            P.op("dve", lambda e, pvb=pvb: e.scalar_tensor_tensor(
                out=ynorm[:], in0=pvb[:, :, 0:64], scalar=0.5,
                in1=rden[:, :].unsqueeze(2).to_broadcast([128, TPB, 64]), op0=ALU.mult, op1=ALU.mult),
                reads=["pv%d" % hb, "rden"], writes=["ynorm"])
            P.op("dve", lambda e, h=h: e.tensor_tensor(
                out=ytok[:, :, h * 64:(h + 1) * 64], in0=ynorm[:], in1=gate[:, :, h * 64:(h + 1) * 64],
                op=ALU.mult),
                reads=["ynorm"] + ["gate%d" % t for t in range(TPB)], writes=["ytok"])
        if h == H - 2:
            evac()
        else:
            pend.append(evac)

    def stage_attn(b, has_prev, nxt):
        first = not has_prev
        units = [unit_v(b, t) for t in range(TPB)] + [unit_za(t) for t in range(TPB)]
        late = []
        for cc in range(4):
            units += [unit_C(cc, first), unit_zc(cc), unit_B(cc), unit_h(cc, late if cc == 3 else None)]
        per_head = [5, 3, 3, 3, 3, 3, 2, 2]
        ui = 0
        pend = []
        for h in range(H):
            for f in pend:
                f()
            pend = []
            bg = []
            if b == 0 and h < 3:
                bg += [lambda n=n: w_item(n) for n in range(16 + 8 * h, 24 + 8 * h)]
                if h == 2 and nxt is not None:
                    bg.append(lambda: load_tile(1, 3))
            if h < 4 and b >= 1:
                bg.append(lambda h=h: out_tile_a(b - 1, h))

                def fin(h=h):
                    out_tile_b(b - 1, h)
                    if nxt is not None:
                        load_tile(nxt, h)
                bg.append(fin)
            if nxt is not None:
                if 4 <= h < 8:
                    ln0_act(nxt, h - 4)
                if 2 <= h < 6:
                    bg.append(lambda h=h: ln0_stats_k(nxt, h - 2, 0))

                    def st2(h=h):
                        ln0_stats_k(nxt, h - 2, 1)
                        ln0_aggr(nxt, h - 2)
                        ln0_small(nxt, h - 2)
                    bg.append(st2)
            streams = [(stage_QK(b, h, has_prev), 2), (chain(units[ui:ui + per_head[h]], bg), 3)]
            ui += per_head[h]
            if h >= 1:
                streams.append((stage_PV(b, h - 1, has_prev, pend), 3))
            interleave(streams)
            for f in bg:
                f()
        for f in pend:
            f()
        pend = []
        assert ui == len(units)
        drive(stage_PV(b, H - 1, has_prev, pend))
        for f in pend:
            f()
        return late

    def stage_out(b, tgen, late=()):
        def tstep():
            if tgen is not None:
                next(tgen, None)
        tstep()
        tstep()
        for cp in range(2):
            i = next_mm()
            for kk in range(2):
                c = 2 * cp + kk
                for t in range(TPB):
                    P.op("pe", lambda e, i=i, c=c, kk=kk, t=t: e.transpose(
                        mmb[i][:, kk * 512 + t * 128: kk * 512 + (t + 1) * 128],
                        ytok[:, t, c * 128:(c + 1) * 128], ident[:]),
                        reads=["ytok", "ident"], writes=["mm%d" % i])
            for kk in range(2):
                c = 2 * cp + kk
                P.op("dve", lambda e, i=i, c=c, kk=kk: e.tensor_copy(
                    out=yT[:, c, :], in_=mmb[i][:, kk * 512:(kk + 1) * 512]),
                    reads=["mm%d" % i], writes=["yT%d" % c])
        for f in late:
            f()
        tstep()
        tstep()
        for t in range(TPB):
            s = slot_of(b, t)
            xr = ["xs%d" % s]
            for hf in range(2):
                i = next_mm()
                for c in range(8):
                    P.op("pe", lambda e, i=i, c=c, t=t, hf=hf: e.matmul(
                        mm[i][:, :], lhsT=yT[:, c, t * 128:(t + 1) * 128],
                        rhs=w_out_sb[:, c, hf * 512:(hf + 1) * 512], start=(c == 0), stop=(c == 7)),
                        reads=["w_out%d" % c, "yT%d" % c], writes=["mm%d" % i])
                P.op("dve", lambda e, i=i, s=s, hf=hf: e.tensor_tensor(
                    out=xs[:, s, hf * 512:(hf + 1) * 512], in0=xs[:, s, hf * 512:(hf + 1) * 512],
                    in1=mm[i][:, :], op=ALU.add),
                    reads=xr + ["mm%d" % i], writes=xr)
                P.op("dve", lambda e, t=t, s=s, hf=hf: e.bn_stats(out=stats1[:, t, hf, :],
                                                                  in_=xs[:, s, hf * 512:(hf + 1) * 512]),
                     reads=xr, writes=["stats1_%d" % t])
            P.op("dve", lambda e, t=t: e.bn_aggr(out=mv1[:, t, :],
                                                 in_=stats1[:, t, :, :].rearrange("p a b -> p (a b)")),
                 reads=["stats1_%d" % t], writes=["mv1"])

    def stage_out_pool(b):
        P.op("pool", lambda e: e.tensor_scalar(out=rstd1[:], in0=mv1[:, :, 1], scalar1=LN_EPS, scalar2=None,
                                               op0=ALU.add),
             reads=["mv1"], writes=["rstd1"])
        P.op("pool", lambda e: e.tensor_tensor(out=rstd1[:], in0=rstd1[:], in1=nhalf[:], op=ALU.pow),
             reads=["rstd1", "nhalf"], writes=["rstd1"])
        P.op("pool", lambda e: e.tensor_scalar(out=nmean1[:], in0=mv1[:, :, 0], scalar1=-1.0, scalar2=None,
                                               op0=ALU.mult),
             reads=["mv1"], writes=["nmean1"])

    def out_tile_a(b, t):
        s = slot_of(b, t)
        xr = ["xs%d" % s]
        P.op("dve", lambda e, t=t, s=s: e.scalar_tensor_tensor(
            out=xs[:, s, :], in0=xs[:, s, :], scalar=nmean1[:, t:t + 1], in1=g1bc[:],
            op0=ALU.add, op1=ALU.mult),
            reads=xr + ["nmean1", "g1bc"], writes=xr)

    def out_tile(b, t):
        out_tile_a(b, t)
        out_tile_b(b, t)

    def out_tile_b(b, t):
        s = slot_of(b, t)
        n = b * TPB + t
        xr = ["xs%d" % s]
        P.op("dve", lambda e, t=t, s=s: e.scalar_tensor_tensor(
            out=xs[:, s, :], in0=xs[:, s, :], scalar=rstd1[:, t:t + 1], in1=b1bc[:],
            op0=ALU.mult, op1=ALU.add),
            reads=xr + ["rstd1", "b1bc"], writes=xr)
        P.dma("sp", lambda e, n=n, s=s: e.dma_start(out=out_d[n], in_=xs[:, s, :]),
              "st%d" % s, reads=xr)

    stage_load(0)
    w_dma(0)
    w_dma(1)
    stage_ln0(0)
    drive(stage_T(0))
    for n in range(8):
        w_item(n)
    drive(stage_P1(0))
    for n in range(8, 16):
        w_item(n)
    if nblk > 1:
        for t in range(3):
            load_tile(1, t)
    for b in range(nblk):
        has_prev = (b % nbs) != 0
        nxt = b + 1 if b + 1 < nblk else None
        late = stage_attn(b, has_prev, nxt)
        stage_out(b, stage_T(nxt) if nxt is not None else None, late)
        stage_out_pool(b)
        if nxt is not None:
            rb = []
            for t in range(TPB):
                rb += [lambda t=t: ln0_res_a(nxt, t), lambda t=t: ln0_res_b(nxt, t)]
            drive(stage_P1(nxt, rb))
            for f in rb:
                f()
    for t in range(TPB):
        out_tile(nblk - 1, t)
    P.wait_all("sp", ["xs%d" % s for s in range(NSLOT)])

    stats = P.emit(nc, es)
    stats["sbuf_free"] = nc.sbuf_bytes_remaining
    stats["prog"] = P
    es.close()
    return nc, stats


def _host_layout(ln0_g, ln0_b, w_in, rel_bias, conv_w, conv_b, w_out, ln_g, ln_b):
    f = np.float32
    kp = np.arange(128)[:, None, None]
    c = np.arange(4)[None, :, None]
    i = np.arange(64)[None, None, :]
    dist = (c - kp // 64) * 64 + i - (kp % 64)
    idx = np.clip(dist, -128, 128) + 128
    rb = np.asarray(rel_bias[0], f)
    rtab = np.ascontiguousarray(rb[:, idx].transpose(1, 0, 2, 3)).reshape(128, 2048)
    rc = np.ascontiguousarray(np.broadcast_to(rb[:, 256][None, :], (128, 8)))
    cw = np.asarray(conv_w[0], f)
    cwl = np.ascontiguousarray(cw.reshape(3, 4, 128).transpose(2, 0, 1)).reshape(128, 12)
    cbl = np.ascontiguousarray(np.asarray(conv_b[0], f).reshape(4, 128).T)
    bc = lambda v: np.ascontiguousarray(np.broadcast_to(np.asarray(v, f).reshape(1, D), (128, D)))
    return {
        "w_in": np.ascontiguousarray(np.asarray(w_in[0], f)),
        "w_out": np.ascontiguousarray(np.asarray(w_out[0], f)),
        "g0T": np.ascontiguousarray(np.asarray(ln0_g, f).reshape(8, 128).T),
        "b0T": np.ascontiguousarray(np.asarray(ln0_b, f).reshape(8, 128).T),
        "g0bc": bc(ln0_g), "b0bc": bc(ln0_b), "g1bc": bc(ln_g[0]), "b1bc": bc(ln_b[0]),
        "cw": cwl, "cb": cbl, "rtab": rtab, "rc": rc,
    }


def kernel(x, ln0_g, ln0_b, w_in, rel_bias, conv_w, conv_b, w_out, ln_g, ln_b):
    x = np.asarray(x, np.float32)
    bsz, seqlen, d = x.shape
    nseq = bsz // N_CORES
    nc, _ = build(nseq, seqlen)
    shared = _host_layout(ln0_g, ln0_b, w_in, rel_bias, conv_w, conv_b, w_out, ln_g, ln_b)
    in_maps = []
    for c in range(N_CORES):
        m = dict(shared)
        m["x"] = np.ascontiguousarray(x[c * nseq:(c + 1) * nseq].reshape(nseq * seqlen, d))
        in_maps.append(m)
    res = run_bass_kernel_spmd(nc, in_maps, core_ids=list(range(N_CORES)))
    outs = [np.asarray(r["out"], np.float32).reshape(nseq, seqlen, d) for r in res.results]
    return np.concatenate(outs, axis=0)
```
